# Optimizing a Trainium2 kernel written in Bass

```python
import math
import jax, jax.numpy as jnp
from jax import lax
import numpy as np


D_MODEL = 4096
BATCH = 4
SEQ = 4096
DEPTH = 2

GRID_W = 64
CTX_LEN = 256
MIX_HALF = D_MODEL // 2

GDN_HEADS = 16
GDN_DK = 128
GDN_DV = 128
GDN_QK = GDN_HEADS * GDN_DK
GDN_V = GDN_HEADS * GDN_DV
GDN_CONV = 3
GDN_CHUNK = 64

DIFF_HEADS = 8
DIFF_D = 128
DIFF_QK = DIFF_HEADS * 2 * DIFF_D
DIFF_V = DIFF_HEADS * 2 * DIFF_D
Q_BLOCK = 128
ROPE_BASE = 10000.0

CONF_CH = MIX_HALF
CONF_K = 31
SC_CH = MIX_HALF
SC_K = 3

D_FF = -(-(8 * D_MODEL) // (3 * 256)) * 256

EVEN_SPLIT = (2 * GDN_QK + GDN_V, GDN_V, 4 * GDN_HEADS, DIFF_QK, DIFF_QK, DIFF_V)
EVEN_IN = sum(EVEN_SPLIT)
EVEN_MIX = GDN_V + DIFF_V
ODD_SPLIT = (CONF_CH, CONF_CH, SC_CH, SC_CH, SC_CH)
ODD_IN = sum(ODD_SPLIT)
ODD_MIX = CONF_CH + SC_CH

F32 = jnp.float32

kernel_name = 'hybrid_gdn_diffattn_conformer_shortconv_prefix_dit'


def _split(p, sizes):
    idx = np.cumsum(sizes)[:-1].tolist()
    return jnp.split(p, idx, axis=-1)


def proj(h, w):
    return jnp.einsum('btd,de->bte', h, w)


def rmsnorm(x, g, eps=1e-6):
    xf = x.astype(F32)
    y = xf * lax.rsqrt(jnp.mean(xf * xf, axis=-1, keepdims=True) + eps)
    return (y * g.astype(F32)).astype(x.dtype)


def layernorm(x, g, b, eps=1e-5):
    xf = x.astype(F32)
    mu = jnp.mean(xf, axis=-1, keepdims=True)
    xc = xf - mu
    y = xc * lax.rsqrt(jnp.mean(xc * xc, axis=-1, keepdims=True) + eps)
    return (y * g.astype(F32) + b.astype(F32)).astype(x.dtype)


def l2norm(x, eps=1e-6):
    return x * lax.rsqrt(jnp.sum(x * x, axis=-1, keepdims=True) + eps)


def modulate(x, shift, scale):
    return x * (1 + scale) + shift


def adaln(cond, w, b):
    m = jnp.einsum('...d,de->...e', jax.nn.silu(cond), w) + b
    return jnp.split(m, 6, axis=-1)


def dwconv(x, w):
    k_w, ch = w.shape
    pad = (k_w - 1) // 2
    return lax.conv_general_dilated(
        x, w.astype(x.dtype)[:, None, :], window_strides=(1,),
        padding=[(pad, k_w - 1 - pad)], dimension_numbers=('NWC', 'WIO', 'NWC'),
        feature_group_count=ch)


def axial_rope_tables(rows, dim):
    half = dim // 2
    inv = 1.0 / (ROPE_BASE ** (jnp.arange(0, half, 2, dtype=F32) / half))
    r = jnp.repeat(jnp.arange(rows), GRID_W).astype(F32)[:, None]
    col = jnp.tile(jnp.arange(GRID_W), rows).astype(F32)[:, None]
    ang = jnp.concatenate([r * inv, r * inv, col * inv, col * inv], axis=-1)
    return jnp.cos(ang), jnp.sin(ang)


def apply_axial_rope(x, cos, sin):
    half = x.shape[-1] // 2
    qt = half // 2
    xf = x.astype(F32)
    rot = lambda t: jnp.concatenate([-t[..., qt:], t[..., :qt]], axis=-1)
    xr = jnp.concatenate([rot(xf[..., :half]), rot(xf[..., half:])], axis=-1)
    return (xf * cos + xr * sin).astype(x.dtype)


def gated_delta_chunked(q, k, v, g, beta, s0):
    bsz, nh, t_len, dk = q.shape
    dv = v.shape[-1]
    n = t_len // GDN_CHUNK
    ch = lambda a: a.reshape(bsz, nh, n, GDN_CHUNK, *a.shape[3:])
    q = ch(q * dk ** -0.5)
    k = ch(k)
    v = ch(v)
    beta = ch(beta)
    g = jnp.cumsum(ch(g), axis=-1)
    incl = jnp.tril(jnp.ones((GDN_CHUNK, GDN_CHUNK), bool))
    strict = jnp.tril(jnp.ones((GDN_CHUNK, GDN_CHUNK), bool), -1)
    gdiff = g[..., :, None] - g[..., None, :]
    decay = jnp.where(incl, jnp.exp(jnp.where(incl, gdiff, 0.0)), 0.0)
    kb = k * beta[..., None]
    lmat = jnp.where(strict, jnp.einsum('bhncd,bhnsd->bhncs', kb, k) * decay, 0.0)
    amat = lmat + jnp.eye(GDN_CHUNK, dtype=F32)
    rhs = jnp.concatenate([v * beta[..., None], kb * jnp.exp(g)[..., None]], axis=-1)
    sol = lax.linalg.triangular_solve(amat, rhs, left_side=True, lower=True, unit_diagonal=True)
    u, w = sol[..., :dv], sol[..., dv:]
    qk = jnp.where(incl, jnp.einsum('bhncd,bhnsd->bhncs', q, k) * decay, 0.0)
    q_dec = q * jnp.exp(g)[..., None]
    g_last = g[..., -1]
    k_tail = k * jnp.exp(g_last[..., None] - g)[..., None]
    front = lambda a: jnp.moveaxis(a, 2, 0)

    def step(s, xs):
        qd_c, qk_c, u_c, w_c, kt_c, gl_c = xs
        v_new = u_c - jnp.einsum('bhck,bhkv->bhcv', w_c, s)
        o_c = jnp.einsum('bhck,bhkv->bhcv', qd_c, s) + jnp.einsum('bhcs,bhsv->bhcv', qk_c, v_new)
        s = s * jnp.exp(gl_c)[..., None, None] + jnp.einsum('bhck,bhcv->bhkv', kt_c, v_new)
        return s, o_c

    xs = (front(q_dec), front(qk), front(u), front(w), front(k_tail), front(g_last))
    s_fin, o = lax.scan(step, s0, xs)
    o = jnp.moveaxis(o, 0, 2).reshape(bsz, nh, t_len, dv)
    return o, s_fin


def gdn_inputs(qkv, ab, conv_w, a_log, dt_bias):
    bsz, t_len, _ = qkv.shape
    qkv = jax.nn.silu(dwconv(qkv, conv_w))
    q, k, v = _split(qkv, (GDN_QK, GDN_QK, GDN_V))
    heads = lambda t, d: t.reshape(bsz, t_len, GDN_HEADS, d).transpose(0, 2, 1, 3).astype(F32)
    q = l2norm(heads(q, GDN_DK))
    k = l2norm(heads(k, GDN_DK))
    v = heads(v, GDN_DV)
    ab = ab.reshape(bsz, t_len, 4, GDN_HEADS).astype(F32)
    g = -jnp.exp(a_log.astype(F32)) * jax.nn.softplus(ab[:, :, :2] + dt_bias.astype(F32))
    beta = jax.nn.sigmoid(ab[:, :, 2:])
    return q, k, v, g.transpose(2, 0, 3, 1), beta.transpose(2, 0, 3, 1)


def gdn_bidir(q, k, v, g, beta, s0_f, s0_b):
    o_f, s_f = gated_delta_chunked(q, k, v, g[0], beta[0], s0_f)
    fl = lambda t: jnp.flip(t, axis=2)
    o_b, s_b = gated_delta_chunked(fl(q), fl(k), fl(v), jnp.flip(g[1], -1), jnp.flip(beta[1], -1), s0_b)
    return o_f + fl(o_b), s_f, s_b


def gdn_out(o, z, gdn_norm):
    bsz, nh, t_len, dv = o.shape
    o = o.transpose(0, 2, 1, 3)
    y = rmsnorm(o, gdn_norm) * jax.nn.silu(z.reshape(bsz, t_len, nh, dv).astype(F32))
    return y.reshape(bsz, t_len, nh * dv).astype(z.dtype)


def diff_heads(dq, dk, dv):
    bsz, t_len, _ = dq.shape
    qk = lambda t: t.reshape(bsz, t_len, DIFF_HEADS, 2, DIFF_D).transpose(0, 2, 3, 1, 4)
    v = dv.reshape(bsz, t_len, DIFF_HEADS, 2 * DIFF_D).transpose(0, 2, 1, 3)
    return qk(dq) * DIFF_D ** -0.5, qk(dk), v


def diff_attention(q, k, v, lam):
    bsz, nh, _, t_len, d = q.shape
    nb = t_len // Q_BLOCK
    qb = jnp.moveaxis(q.reshape(bsz, nh, 2, nb, Q_BLOCK, d), 3, 0)
    vf = v.astype(F32)

    def block(qi):
        s = jnp.einsum('bhmqd,bhmkd->bhmqk', qi, k, preferred_element_type=F32)
        p = jax.nn.softmax(s, axis=-1)
        a = p[:, :, 0] - lam * p[:, :, 1]
        return jnp.einsum('bhqk,bhke->bhqe', a, vf)

    o = lax.map(block, qb)
    return jnp.moveaxis(o, 0, 2).reshape(bsz, nh, t_len, 2 * d)


def diff_out(o, diff_norm, lambda_init, dtype):
    bsz, nh, t_len, e = o.shape
    y = rmsnorm(o, diff_norm, eps=1e-5) * (1.0 - lambda_init)
    return y.transpose(0, 2, 1, 3).reshape(bsz, t_len, nh * e).astype(dtype)


def mixer_ab(xn, cn, cos, sin, w_in, w_out, conv_w, a_log, dt_bias, gdn_norm, lam_vecs, diff_norm,
             lambda_init, need_ctx):
    px = proj(xn, w_in)
    pc = proj(cn, w_in)
    qkv_x, z_x, ab_x, dq_x, dk_x, dv_x = _split(px, EVEN_SPLIT)
    qkv_c, z_c, ab_c, dq_c, dk_c, dv_c = _split(pc, EVEN_SPLIT)
    qa_c, ka_c, va_c, g_c, b_c = gdn_inputs(qkv_c, ab_c, conv_w, a_log, dt_bias)
    qa_x, ka_x, va_x, g_x, b_x = gdn_inputs(qkv_x, ab_x, conv_w, a_log, dt_bias)
    s0 = jnp.zeros((cn.shape[0], GDN_HEADS, GDN_DK, GDN_DV), F32)
    oa_c, s_f, s_b = gdn_bidir(qa_c, ka_c, va_c, g_c, b_c, s0, s0)
    oa_x, _, _ = gdn_bidir(qa_x, ka_x, va_x, g_x, b_x, s_f, s_b)
    lv = lam_vecs.astype(F32)
    lam = jnp.exp(jnp.sum(lv[0] * lv[1])) - jnp.exp(jnp.sum(lv[2] * lv[3])) + lambda_init
    qb_c, kb_c, vb_c = diff_heads(dq_c, dk_c, dv_c)
    qb_x, kb_x, vb_x = diff_heads(dq_x, dk_x, dv_x)
    qb_x = apply_axial_rope(qb_x, cos, sin)
    kb_x = apply_axial_rope(kb_x, cos, sin)
    k_all = jnp.concatenate([kb_x, kb_c], axis=3)
    v_all = jnp.concatenate([vb_x, vb_c], axis=2)
    ob_x = diff_attention(qb_x, k_all, v_all, lam)
    y_x = proj(jnp.concatenate([gdn_out(oa_x, z_x, gdn_norm),
                                diff_out(ob_x, diff_norm, lambda_init, xn.dtype)], axis=-1), w_out)
    y_c = None
    if need_ctx:
        ob_c = diff_attention(qb_c, kb_c, vb_c, lam)
        y_c = proj(jnp.concatenate([gdn_out(oa_c, z_c, gdn_norm),
                                    diff_out(ob_c, diff_norm, lambda_init, cn.dtype)], axis=-1), w_out)
    return y_x, y_c


def mixer_cd(h, w_in, w_out, conf_dw, conf_dw_b, conf_ln_g, conf_ln_b, sc_conv):
    p = proj(h, w_in)
    glu_a, glu_b, gate_b, gate_c, sh = _split(p, ODD_SPLIT)
    yc = glu_a * jax.nn.sigmoid(glu_b)
    yc = dwconv(yc, conf_dw) + conf_dw_b
    yc = jax.nn.silu(layernorm(yc, conf_ln_g, conf_ln_b))
    yd = gate_b * dwconv(gate_c * sh, sc_conv)
    return proj(jnp.concatenate([yc, yd], axis=-1), w_out)


def swiglu(h, wg, wu, wd):
    return proj(jax.nn.silu(proj(h, wg)) * proj(h, wu), wd)


def setup_inputs(seed: int = 0) -> dict:
    key = jax.random.key(seed)
    ks = iter(jax.random.split(key, 32))
    nrm = lambda shape, scale: jax.random.normal(next(ks), shape, F32) * scale
    D = D_MODEL
    ne = (DEPTH + 1) // 2
    no = DEPTH // 2
    inp = {}
    inp['x'] = nrm((BATCH, SEQ, D), 1.0)
    inp['c'] = nrm((BATCH, D), 1.0)
    inp['ctx'] = nrm((BATCH, CTX_LEN, D), 1.0)
    inp['c_ctx'] = nrm((D,), 1.0)
    inp['w_ada'] = nrm((DEPTH, D, 6 * D), 0.5 * D ** -0.5)
    inp['b_ada'] = nrm((DEPTH, 6 * D), 0.02)
    inp['g_mix_pre'] = 1.0 + nrm((DEPTH, D), 0.05)
    inp['g_mix_post'] = 1.0 + nrm((DEPTH, D), 0.05)
    inp['g_ffn_pre'] = 1.0 + nrm((DEPTH, D), 0.05)
    inp['g_ffn_post'] = 1.0 + nrm((DEPTH, D), 0.05)
    inp['w_ff_gate'] = nrm((DEPTH, D, D_FF), D ** -0.5)
    inp['w_ff_up'] = nrm((DEPTH, D, D_FF), D ** -0.5)
    inp['w_ff_down'] = nrm((DEPTH, D_FF, D), D_FF ** -0.5)
    inp['w_in_even'] = nrm((ne, D, EVEN_IN), D ** -0.5)
    inp['w_out_even'] = nrm((ne, EVEN_MIX, D), EVEN_MIX ** -0.5)
    inp['gdn_conv'] = nrm((ne, GDN_CONV, 2 * GDN_QK + GDN_V), GDN_CONV ** -0.5)
    inp['gdn_a_log'] = jnp.log(jax.random.uniform(next(ks), (ne, 2, GDN_HEADS), F32, 1.0, 16.0))
    dt = jnp.exp(jax.random.uniform(next(ks), (ne, 2, GDN_HEADS), F32, math.log(1e-3), math.log(1e-1)))
    inp['gdn_dt_bias'] = dt + jnp.log(-jnp.expm1(-dt))
    inp['gdn_norm'] = 1.0 + nrm((ne, GDN_DV), 0.05)
    inp['diff_lambda'] = nrm((ne, 4, DIFF_D), 0.1)
    inp['diff_norm'] = 1.0 + nrm((ne, 2 * DIFF_D), 0.05)
    inp['w_in_odd'] = nrm((no, D, ODD_IN), D ** -0.5)
    inp['w_out_odd'] = nrm((no, ODD_MIX, D), ODD_MIX ** -0.5)
    inp['conf_dw'] = nrm((no, CONF_K, CONF_CH), CONF_K ** -0.5)
    inp['conf_dw_b'] = nrm((no, CONF_CH), 0.02)
    inp['conf_ln_g'] = 1.0 + nrm((no, CONF_CH), 0.05)
    inp['conf_ln_b'] = nrm((no, CONF_CH), 0.02)
    inp['sc_conv'] = nrm((no, SC_K, SC_CH), SC_K ** -0.5)
    return inp


def reference(x, c, ctx, c_ctx, w_ada, b_ada, g_mix_pre, g_mix_post, g_ffn_pre, g_ffn_post,
              w_ff_gate, w_ff_up, w_ff_down, w_in_even, w_out_even, gdn_conv, gdn_a_log,
              gdn_dt_bias, gdn_norm, diff_lambda, diff_norm, w_in_odd, w_out_odd, conf_dw,
              conf_dw_b, conf_ln_g, conf_ln_b, sc_conv):
    n_lat = x.shape[1]
    rows = n_lat // GRID_W
    cos, sin = axial_rope_tables(rows, DIFF_D)
    h = ctx
    for i in range(DEPTH):
        last = i == DEPTH - 1
        j = i // 2
        even = i % 2 == 0
        mx = [m[:, None, :] for m in adaln(c, w_ada[i], b_ada[i])]
        mc = adaln(c_ctx, w_ada[i], b_ada[i])
        xn = modulate(rmsnorm(x, g_mix_pre[i]), mx[0], mx[1])
        y_c = None
        if even:
            cn = modulate(rmsnorm(h, g_mix_pre[i]), mc[0], mc[1])
            y_x, y_c = mixer_ab(xn, cn, cos, sin, w_in_even[j], w_out_even[j], gdn_conv[j],
                                gdn_a_log[j], gdn_dt_bias[j], gdn_norm[j], diff_lambda[j],
                                diff_norm[j], 0.8 - 0.6 * math.exp(-0.3 * i), not last)
        else:
            y_x = mixer_cd(xn, w_in_odd[j], w_out_odd[j], conf_dw[j], conf_dw_b[j],
                           conf_ln_g[j], conf_ln_b[j], sc_conv[j])
            if not last:
                cn = modulate(rmsnorm(h, g_mix_pre[i]), mc[0], mc[1])
                y_c = mixer_cd(cn, w_in_odd[j], w_out_odd[j], conf_dw[j], conf_dw_b[j],
                               conf_ln_g[j], conf_ln_b[j], sc_conv[j])
        x = x + mx[2] * rmsnorm(y_x, g_mix_post[i])
        f_x = swiglu(modulate(rmsnorm(x, g_ffn_pre[i]), mx[3], mx[4]), w_ff_gate[i], w_ff_up[i], w_ff_down[i])
        x = x + mx[5] * rmsnorm(f_x, g_ffn_post[i])
        if not last:
            h = h + mc[2] * rmsnorm(y_c, g_mix_post[i])
            f_c = swiglu(modulate(rmsnorm(h, g_ffn_pre[i]), mc[3], mc[4]), w_ff_gate[i], w_ff_up[i], w_ff_down[i])
            h = h + mc[5] * rmsnorm(f_c, g_ffn_post[i])
    return x
```

```python
from contextlib import ExitStack
import numpy as np
import concourse.bass as bass
import concourse.mybir as mybir

F32 = mybir.dt.float32
BF16 = mybir.dt.bfloat16
AF = mybir.ActivationFunctionType
ALU = mybir.AluOpType
AX = mybir.AxisListType
ENGS = ["pe", "act", "dve", "pool", "sp"]
SAME_ENG_SYNC = True


class Sem:
    def __init__(self, h, name):
        self.h = h
        self.count = 0
        self.name = name


class Res:
    def __init__(self, name=""):
        self.name = name
        self.last_write = None
        self.readers = []

    def rdeps(self):
        return [self.last_write] if self.last_write else []

    def wdeps(self):
        d = list(self.readers)
        if self.last_write:
            d.append(self.last_write)
        return d


class Tile(Res):
    def __init__(self, t, name, sem=None):
        super().__init__(name)
        self.t = t
        self.sem = sem

    def __getitem__(self, k):
        return self.t[k]


class Ring:
    def __init__(self, tiles):
        self.tiles = tiles
        self.i = 0

    def next(self):
        t = self.tiles[self.i % len(self.tiles)]
        self.i += 1
        return t


class Prog:
    def __init__(self, nc, es, n_dma_sems=40):
        self.nc = nc
        self.esem = {e: Sem(es.enter_context(nc.semaphore("s_" + e)), e) for e in ENGS}
        self.free_dsems = [Sem(es.enter_context(nc.semaphore("d%d" % i)), "d%d" % i)
                           for i in range(n_dma_sems)]
        self.uid = 0
        self.n_inst = 0
        self.phase_idx = 0
        self.max_phases = None
        self.phase_names = []


class Phase:
    def __init__(self, prog, name):
        self.p = prog
        self.nc = prog.nc
        self.name = name
        self.es = ExitStack()
        self.q = {e: [] for e in ENGS}
        self.seen = {e: {} for e in ENGS}
        self.used_dsems = []
        self.pending = {}

    def tile(self, shape, dtype, name=None, dma=False):
        self.p.uid += 1
        nm = "%s_%s_%d" % (self.name, name or "t", self.p.uid)
        t = self.es.enter_context(self.nc.sbuf_tensor(nm, list(shape), dtype))
        sem = None
        if dma:
            sem = self.p.free_dsems.pop()
            self.used_dsems.append(sem)
        return Tile(t, nm, sem)

    def ring(self, n, shape, dtype, name=None, dma=False):
        return Ring([self.tile(shape, dtype, (name or "r") + str(i), dma) for i in range(n)])

    def psum(self, n=8, shape=(128, 512), dtype=F32):
        tl = []
        for i in range(n):
            self.p.uid += 1
            nm = "%s_ps_%d" % (self.name, self.p.uid)
            t = self.es.enter_context(self.nc.psum_tensor(nm, list(shape), dtype))
            tl.append(Tile(t, nm))
        return tl

    def _waits(self, e, deps):
        w = []
        for (sem, val) in deps:
            if sem is self.p.esem[e] and (e == "pe" or not SAME_ENG_SYNC):
                continue
            if self.seen[e].get(sem.name, 0) < val:
                self.seen[e][sem.name] = val
                w.append((sem, val))
        return w

    def op(self, e, fn, reads=(), writes=(), inc=True):
        deps = []
        for r in reads:
            deps += r.rdeps()
        for w in writes:
            deps += w.wdeps()
        waits = self._waits(e, deps)
        sem = self.p.esem[e]
        pend = self.pending.setdefault(e, ([], []))
        pend[0].extend(reads)
        pend[1].extend(writes)
        if inc:
            sem.count += 1
            tok = (sem, sem.count)
            for r in pend[0]:
                r.readers.append(tok)
            for w in pend[1]:
                w.last_write = tok
                w.readers = []
            self.pending[e] = ([], [])
        self.q[e].append((waits, fn, sem.h if inc else None, 1))
        self.p.n_inst += 1

    def dma(self, e, out, in_, sem_tile, reads=(), writes=(), group=False):
        deps = []
        for r in reads:
            deps += r.rdeps()
        for w in writes:
            deps += w.wdeps()
        sem = sem_tile.sem
        if group:
            deps = [d for d in deps if d[0] is not sem]
        waits = self._waits(e, deps)
        sem.count += 16
        tok = (sem, sem.count)
        for r in reads:
            r.readers.append(tok)
        for w in writes:
            w.last_write = tok
            w.readers = []
        self.q[e].append((waits, lambda eng: eng.dma_start(out=out, in_=in_), sem.h, 16))
        self.p.n_inst += 1

    def simulate(self):
        vals = dict(getattr(self.p, "simvals", {}))
        pos = {e: 0 for e in ENGS}
        prog = True
        while prog:
            prog = False
            for e in ENGS:
                lst = self.q[e]
                while pos[e] < len(lst):
                    waits, fn, semh, n = lst[pos[e]]
                    if all(vals.get(id(s_.h), 0) >= v for (s_, v) in waits):
                        if semh is not None:
                            vals[id(semh)] = vals.get(id(semh), 0) + n
                        pos[e] += 1
                        prog = True
                    else:
                        break
        stuck = {e: (pos[e], len(self.q[e])) for e in ENGS if pos[e] < len(self.q[e])}
        if stuck:
            print("DEADLOCK in phase", self.name, stuck)
            for e in stuck:
                waits = self.q[e][pos[e]][0]
                print("  ", e, "waits", [(s_.name, v, vals.get(id(s_.h), 0)) for (s_, v) in waits])
        self.p.simvals = vals

    def finish(self):
        nc = self.nc
        self.p.phase_names.append(self.name)
        self.p.phase_idx += 1
        if (self.p.max_phases is not None and self.p.phase_idx > self.p.max_phases) or \
                (getattr(self.p, "min_phase", 0) >= self.p.phase_idx):
            for s in self.used_dsems:
                self.p.free_dsems.append(s)
            self.es.close()
            return
        finals = [(s, s.count) for s in self.p.esem.values() if s.count > 0]
        finals += [(s, s.count) for s in self.used_dsems if s.count > 0]
        for e in ENGS:
            w = self._waits(e, [f for f in finals if f[0] is not self.p.esem[e]])
            self.q[e].append((w, None, None, 0))
        q = self.q
        if getattr(self.p, "simulate", False):
            self.simulate()

        def replay(lst, eng):
            for (waits, fn, semh, n) in lst:
                for (s, v) in waits:
                    eng.wait_ge(s.h, v)
                if fn is not None:
                    ins = fn(eng)
                    if semh is not None:
                        ins.then_inc(semh, n)

        with nc.Block() as block:
            @block.tensor
            def _(eng):
                replay(q["pe"], eng)

            @block.scalar
            def _(eng):
                replay(q["act"], eng)

            @block.vector
            def _(eng):
                replay(q["dve"], eng)

            @block.gpsimd
            def _(eng):
                replay(q["pool"], eng)

            @block.sync
            def _(eng):
                replay(q["sp"], eng)
        for s in self.used_dsems:
            self.p.free_dsems.append(s)
        self.es.close()


def dram_fm(ap2d, t0, ts):
    return ap2d.rearrange("(j p) t -> p j t", p=128)[:, :, t0:t0 + ts]


def norm_phase(prog, name, src, T, out, avec, bvec=None, resid=None, eps=1e-6, ts=None, D=4096,
               out_dtype=None, ocol0=0):
    ph = Phase(prog, name)
    nc = ph.nc
    if out_dtype is None:
        out_dtype = BF16 if resid is None else F32
    if ts is None:
        ts = 272 if resid is None else 136
    KC = D // 128
    nt = T // ts
    assert nt * ts == T
    ones = ph.tile([128, 128], BF16, "ones")
    at = ph.tile([128, KC], F32, "a", dma=True)
    bt = ph.tile([128, KC], F32, "b", dma=True) if bvec is not None else None
    xr = ph.ring(2, [128, KC, ts], F32, "x", dma=True)
    rr = ph.ring(2, [128, KC, ts], F32, "res", dma=True) if resid is not None else None
    sq = ph.ring(3, [128, ts], BF16, "sq")
    tmp = ph.ring(3, [128, ts], F32, "tmp")
    rstd = ph.ring(2, [128, ts], F32, "rstd")
    orr = ph.ring(2, [128, KC, ts], out_dtype, "o", dma=True)
    ps = ph.psum(2, (128, 512))
    ph.op("pool", lambda e: e.memset(ones[:], 1.0), writes=[ones])
    ph.dma("sp", at[:], avec, at, writes=[at])
    if bt is not None:
        ph.dma("sp", bt[:], bvec, bt, writes=[bt])
    for i in range(nt):
        t0 = i * ts
        x = xr.next()
        ph.dma("sp", x[:], dram_fm(src, t0, ts), x, writes=[x])
        if rr is not None:
            rs = rr.next()
            ph.dma("sp", rs[:], dram_fm(resid, t0, ts), rs, writes=[rs])
        p = ps[i % 2]
        for j in range(KC):
            s = sq.next()
            ph.op("act", lambda e, s=s, x=x, j=j: e.activation(out=s[:], in_=x[:, j, :], func=AF.Square),
                  reads=[x], writes=[s])
            ph.op("pe", lambda e, p=p, s=s, j=j: e.matmul(p[:, :ts], ones[:], s[:], start=(j == 0), stop=(j == KC - 1)),
                  reads=[s, ones], writes=[p], inc=(j == KC - 1))
        r = rstd.next()
        ph.op("dve", lambda e, r=r, p=p: e.tensor_scalar(out=r[:], in0=p[:, :ts], scalar1=1.0 / D, scalar2=eps,
                                                         op0=ALU.mult, op1=ALU.add), reads=[p], writes=[r])
        ph.op("act", lambda e, r=r: e.activation(out=r[:], in_=r[:], func=AF.Sqrt), reads=[r], writes=[r])
        ph.op("dve", lambda e, r=r: e.reciprocal(out=r[:], in_=r[:]), reads=[r], writes=[r])
        o = orr.next()
        for j in range(KC):
            t = tmp.next()
            ph.op("dve", lambda e, t=t, x=x, j=j, r=r: e.scalar_tensor_tensor(
                out=t[:], in0=x[:, j, :], scalar=at[:, j:j + 1], in1=r[:], op0=ALU.mult, op1=ALU.mult),
                reads=[x, at, r], writes=[t])
            if resid is None:
                ph.op("act", lambda e, t=t, o=o, j=j: e.activation(out=o[:, j, :], in_=t[:], func=AF.Identity,
                                                                  bias=bt[:, j:j + 1], scale=1.0),
                      reads=[t, bt], writes=[o])
            else:
                ph.op("pool", lambda e, t=t, o=o, j=j, rs=rs: e.tensor_tensor(out=o[:, j, :], in0=t[:], in1=rs[:, j, :],
                                                                            op=ALU.add),
                      reads=[t, rs], writes=[o])
        ph.dma("sp", dram_fm(out, ocol0 + t0, ts), o[:], o, reads=[o])
    ph.finish()


def gemm_fm_phase(prog, name, act, T, subs, K, wsrc, nblocks, NB, epi, extra=None):
    ph = Phase(prog, name)
    KC = K // 128
    a = ph.tile([128, KC, T], BF16, "act", dma=True)
    G = 8
    for g in range(0, KC, G):
        ge = min(KC, g + G)
        ph.dma("sp", a[:, g:ge, :], act.rearrange("(j p) t -> p j t", p=128)[:, g:ge, 0:T], a, writes=[a], group=(g > 0))
    wr = ph.ring(2, [128, KC, NB], BF16, "w", dma=True)
    ps = ph.psum(8, (128, 512))
    st = extra(ph) if extra else None
    pi = 0
    for gi, n0 in enumerate(nblocks):
        w = wr.next()
        half = KC // 2
        ph.dma("pool", w[:, :half, :], wsrc(n0, NB).rearrange("(j p) n -> p j n", p=128)[:, :half, :], w, writes=[w])
        ph.dma("pool", w[:, half:, :], wsrc(n0, NB).rearrange("(j p) n -> p j n", p=128)[:, half:, :], w, writes=[w], group=True)
        for j in range(NB // 128):
            for si, (c0, cs) in enumerate(subs):
                p = ps[pi % 8]
                pi += 1
                for kc in range(KC):
                    ph.op("pe", lambda e, p=p, w=w, kc=kc, j=j, c0=c0, cs=cs: e.matmul(
                        p[:, :cs], w[:, kc, j * 128:(j + 1) * 128], a[:, kc, c0:c0 + cs],
                        start=(kc == 0), stop=(kc == KC - 1)),
                        reads=[w, a], writes=[p], inc=(kc == KC - 1))
                epi(ph, st, gi, n0, j, si, p, c0, cs)
    ph.finish()


def gemm_phase(prog, name, act, acol0, T, K, wsrc, blocks, subs, extra=None, NBW=256, post=None):
    ph = Phase(prog, name)
    KC = K // 128
    a = ph.tile([128, KC, T], BF16, "act", dma=True)
    G = 8
    for g in range(0, KC, G):
        ge = min(KC, g + G)
        ph.dma("sp", a[:, g:ge, :], act.rearrange("(j p) t -> p j t", p=128)[:, g:ge, acol0:acol0 + T], a, writes=[a],
               group=(g > 0))
    wr = ph.ring(2, [128, KC, NBW], BF16, "w", dma=True)
    ps = ph.psum(8, (128, 512))
    st = extra(ph) if extra else None
    pi = 0
    for (n0, nb, mode, epi) in blocks:
        w = wr.next()
        wv = wsrc(n0, nb).rearrange("(j p) n -> p j n", p=128)
        for g0 in range(0, KC, 16):
            g1 = min(KC, g0 + 16)
            ph.dma("pool", w[:, g0:g1, :nb], wv[:, g0:g1, :], w, writes=[w], group=(g0 > 0))
        if mode == "fm":
            for j in range(nb // 128):
                for si, (c0, cs) in enumerate(subs):
                    p = ps[pi % 8]
                    pi += 1
                    for kc in range(KC):
                        ph.op("pe", lambda e, p=p, w=w, kc=kc, j=j, c0=c0, cs=cs: e.matmul(
                            p[:, :cs], w[:, kc, j * 128:(j + 1) * 128], a[:, kc, c0:c0 + cs],
                            start=(kc == 0), stop=(kc == KC - 1)),
                            reads=[w, a], writes=[p], inc=(kc == KC - 1))
                    epi(ph, st, n0, j, si, p, c0, cs)
        else:
            for ti in range(T // 128):
                p = ps[pi % 8]
                pi += 1
                for kc in range(KC):
                    ph.op("pe", lambda e, p=p, w=w, kc=kc, ti=ti, nb=nb: e.matmul(
                        p[:, :nb], a[:, kc, ti * 128:(ti + 1) * 128], w[:, kc, :nb],
                        start=(kc == 0), stop=(kc == KC - 1)),
                        reads=[w, a], writes=[p], inc=(kc == KC - 1))
                epi(ph, st, n0, ti, p, nb)
    if post:
        post(ph, st)
    ph.finish()

C = 128


def gdn_prep(prog, name, HG, TS, TO, TC, d):
    ph = Phase(prog, name)
    TM = max(TS, TC, TO)
    cw = ph.tile([128, 3 * HG * 3], F32, "cw", dma=True)
    ident = ph.tile([128, 128], F32, "ident", dma=True)
    ones = ph.tile([128, 128], BF16, "ones")
    ph.dma("sp", cw[:], d["convw"], cw, writes=[cw])
    ph.dma("sp", ident[:], d["ident"], ident, writes=[ident])
    ph.op("pool", lambda e: e.memset(ones[:], 1.0), writes=[ones])
    xin = ph.ring(2, [128, TM + 2], F32, "xin", dma=True)
    yr = ph.ring(2, [128, TM], F32, "y", dma=True)
    sqr = ph.ring(2, [128, 512], BF16, "sq")
    rr = ph.ring(2, [128, 512], F32, "rinv")
    tokr = ph.ring(2, [128, TM // 128, 128], F32, "tok", dma=True)
    ps = ph.psum(8)
    pi = [0]

    def nps():
        p = ps[pi[0] % 8]
        pi[0] += 1
        return p

    for h in range(HG):
        for (X, xi, segs) in (("k", 1, (("gk_pre", TS, 0), ("ck_pre", TC, TS))),
                              ("v", 2, (("gv_pre", TS, 0), ("cv_pre", TC, TS))),
                              ("q", 0, (("gq_pre", TO, 0),))):
            for (src, T, off) in segs:
                x = xin.next()
                y = yr.next()
                ph.op("pool", lambda e, x=x: e.memset(x[:, 0:1], 0.0), writes=[x])
                ph.op("pool", lambda e, x=x, T=T: e.memset(x[:, T + 1:T + 2], 0.0), writes=[x])
                ph.dma("sp", x[:, 1:T + 1], d[src][h * 128:(h + 1) * 128, 0:T], x, writes=[x])
                wc = lambda k, xi=xi, h=h: cw[:, (xi * HG + h) * 3 + k:(xi * HG + h) * 3 + k + 1]
                ph.op("act", lambda e, x=x, y=y, T=T, wc=wc: e.activation(out=y[:, :T], in_=x[:, 1:T + 1], func=AF.Copy,
                                                                       scale=wc(1)), reads=[x, cw], writes=[y])
                ph.op("dve", lambda e, x=x, y=y, T=T, wc=wc: e.scalar_tensor_tensor(
                    out=y[:, :T], in0=x[:, 0:T], scalar=wc(0), in1=y[:, :T], op0=ALU.mult, op1=ALU.add),
                    reads=[x, y, cw], writes=[y])
                ph.op("dve", lambda e, x=x, y=y, T=T, wc=wc: e.scalar_tensor_tensor(
                    out=y[:, :T], in0=x[:, 2:T + 2], scalar=wc(2), in1=y[:, :T], op0=ALU.mult, op1=ALU.add),
                    reads=[x, y, cw], writes=[y])
                ph.op("act", lambda e, y=y, T=T: e.activation(out=y[:, :T], in_=y[:, :T], func=AF.Silu),
                      reads=[y], writes=[y])
                if X in ("k", "q"):
                    for c0 in range(0, T, 512):
                        cs = min(512, T - c0)
                        s = sqr.next()
                        p = nps()
                        r = rr.next()
                        ph.op("act", lambda e, s=s, y=y, c0=c0, cs=cs: e.activation(out=s[:, :cs], in_=y[:, c0:c0 + cs],
                                                                                 func=AF.Square), reads=[y], writes=[s])
                        ph.op("pe", lambda e, p=p, s=s, cs=cs: e.matmul(p[:, :cs], ones[:], s[:, :cs], start=True, stop=True),
                              reads=[s, ones], writes=[p])
                        ph.op("dve", lambda e, r=r, p=p, cs=cs: e.tensor_scalar(out=r[:, :cs], in0=p[:, :cs], scalar1=1e-6,
                                                                              scalar2=1.0, op0=ALU.add, op1=ALU.mult), reads=[p], writes=[r])
                        ph.op("act", lambda e, r=r, cs=cs: e.activation(out=r[:, :cs], in_=r[:, :cs], func=AF.Sqrt),
                              reads=[r], writes=[r])
                        ph.op("dve", lambda e, r=r, cs=cs: e.reciprocal(out=r[:, :cs], in_=r[:, :cs]), reads=[r], writes=[r])
                        if X == "q":
                            ph.op("dve", lambda e, r=r, y=y, c0=c0, cs=cs: e.scalar_tensor_tensor(
                                out=y[:, c0:c0 + cs], in0=y[:, c0:c0 + cs], scalar=float(C) ** -0.5, in1=r[:, :cs],
                                op0=ALU.mult, op1=ALU.mult), reads=[y, r], writes=[y])
                        else:
                            ph.op("dve", lambda e, r=r, y=y, c0=c0, cs=cs: e.tensor_tensor(
                                out=y[:, c0:c0 + cs], in0=y[:, c0:c0 + cs], in1=r[:, :cs], op=ALU.mult),
                                reads=[y, r], writes=[y])
                if X == "k":
                    ph.dma("sp", d["KT"][h * 128:(h + 1) * 128, off:off + T], y[:, :T], y, reads=[y])
                if X == "q":
                    ph.dma("sp", d["QT"][h * 128:(h + 1) * 128, 0:T], y[:, :T], y, reads=[y])
                if X in ("k", "v"):
                    tk = tokr.next()
                    for c in range(T // 128):
                        p = nps()
                        ph.op("pe", lambda e, p=p, y=y, c=c: e.transpose(p[:, :128], y[:, c * 128:(c + 1) * 128], ident[:]),
                              reads=[y, ident], writes=[p])
                        ph.op("act" if c % 2 else "dve",
                              (lambda e, p=p, tk=tk, c=c: e.copy(out=tk[:, c, :], in_=p[:, :128])) if c % 2 else
                              (lambda e, p=p, tk=tk, c=c: e.tensor_copy(out=tk[:, c, :], in_=p[:, :128])),
                              reads=[p], writes=[tk])
                    dst = d["Ktok"] if X == "k" else d["Vtok"]
                    ph.dma("sp", dst[h, off:off + T, :].rearrange("(c p) f -> p c f", p=128), tk[:, :T // 128, :], tk, reads=[tk])
    al = ph.tile([128, 32], F32, "al", dma=True)
    db = ph.tile([128, 32], F32, "db", dma=True)
    ph.dma("sp", al[:], d["alog"], al, writes=[al])
    ph.dma("sp", db[:], d["dtb"], db, writes=[db])
    ph.op("act", lambda e: e.activation(out=al[:], in_=al[:], func=AF.Exp), reads=[al], writes=[al])
    abr = ph.ring(2, [128, 64], F32, "ab", dma=True)
    scr = ph.ring(2, [128, 64], F32, "sc", dma=True)
    for (src, T, off) in (("ab_tok", TS, 0), ("abc_tok", TC, TS)):
        for c in range(T // 128):
            a = abr.next()
            s = scr.next()
            ph.dma("sp", a[:], d[src][c * 128:(c + 1) * 128, :], a, writes=[a])
            ph.op("dve", lambda e, a=a, s=s: e.tensor_tensor(out=s[:, 0:32], in0=a[:, 0:32], in1=db[:], op=ALU.add),
                  reads=[a, db], writes=[s])
            ph.op("act", lambda e, s=s: e.activation(out=s[:, 0:32], in_=s[:, 0:32], func=AF.Exp), reads=[s], writes=[s])
            ph.op("act", lambda e, s=s: e.activation(out=s[:, 0:32], in_=s[:, 0:32], func=AF.Ln, bias=1.0, scale=1.0),
                  reads=[s], writes=[s])
            ph.op("dve", lambda e, s=s: e.scalar_tensor_tensor(out=s[:, 0:32], in0=s[:, 0:32], scalar=-1.0, in1=al[:],
                                                              op0=ALU.mult, op1=ALU.mult), reads=[s, al], writes=[s])
            ph.op("act", lambda e, a=a, s=s: e.activation(out=s[:, 32:64], in_=a[:, 32:64], func=AF.Sigmoid),
                  reads=[a, s], writes=[s])
            ph.dma("sp", d["sc_tok"][off + c * 128: off + (c + 1) * 128, :], s[:], s, reads=[s])
    ph.finish()


def gdn_scan(prog, name, HG, TS, TO, TC, d):
    ph = Phase(prog, name)
    NS, NO, NCC = TS // C, TO // C, TC // C
    NT = NS + NCC
    cst = {}
    for nm in ("ident", "Uf", "Ub", "MSf", "MSb", "MIf", "MIb"):
        t = ph.tile([128, 128], F32, nm, dma=True)
        ph.dma("sp", t[:], d[nm], t, writes=[t])
        cst[nm] = t
    ident = cst["ident"]
    onesf = ph.tile([128, 128], F32, "onesf")
    onesb = ph.tile([128, 128], BF16, "onesb")
    ph.op("pool", lambda e: e.memset(onesf[:], 1.0), writes=[onesf])
    ph.op("pool", lambda e: e.memset(onesb[:], 1.0), writes=[onesb])
    gn = ph.tile([128, 1], F32, "gn", dma=True)
    ph.dma("sp", gn[:], d["gnorm"], gn, writes=[gn])
    sc = ph.tile([128, NT, 64], F32, "sc", dma=True)
    ph.dma("sp", sc[:], d["sc_tok"].rearrange("(c p) f -> p c f", p=128), sc, writes=[sc])
    KTr = ph.ring(1, [128, TS + TC], F32, "KT", dma=True)
    QTr = ph.ring(1, [128, TO], F32, "QT", dma=True)
    Ktr = ph.ring(1, [128, NT, 128], F32, "Ktok", dma=True)
    Vtr = ph.ring(1, [128, NT, 128], F32, "Vtok", dma=True)
    zr = ph.ring(2, [128, TO], F32, "z", dma=True)
    oT = ph.ring(2, [128, TO], F32, "oT")
    mixo = ph.ring(2, [128, TO], BF16, "mixo", dma=True)
    Sr = ph.ring(2, [128, 128], F32, "S")
    ps = ph.psum(8)
    pi = [0]

    def nps():
        p = ps[pi[0] % 8]
        pi[0] += 1
        return p

    def R(nm, n=2, shape=(128, 128)):
        return ph.ring(n, list(shape), F32, nm)
    rg = {nm: R(nm) for nm in ("Gs", "tp", "tn", "D", "DT", "grow", "B", "A", "Rm", "bv", "kbg", "ktl", "u", "wT", "qk",
                               "qd", "vn", "o1")}
    smr = ph.ring(3, [128, 8], F32, "small")
    sqr = ph.ring(2, [128, 512], BF16, "sq")
    rsr = ph.ring(2, [128, 512], F32, "rs")

    evi = [0]

    def evac(dst_ap, src_ap, reads, writes):
        evi[0] += 1
        if evi[0] % 2:
            ph.op("act", lambda e: e.copy(out=dst_ap, in_=src_ap), reads=reads, writes=writes)
        else:
            ph.op("dve", lambda e: e.tensor_copy(out=dst_ap, in_=src_ap), reads=reads, writes=writes)

    def mm(p, lhsT, rhs, reads, n=128):
        ph.op("pe", lambda e: e.matmul(p[:, :n], lhsT, rhs, start=True, stop=True), reads=reads, writes=[p])

    for h in range(HG):
        KT = KTr.next(); QT = QTr.next(); Kt = Ktr.next(); Vt = Vtr.next(); z = zr.next(); o = oT.next()
        ph.dma("sp", KT[:], d["KT"][h * 128:(h + 1) * 128, :], KT, writes=[KT])
        ph.dma("sp", QT[:], d["QT"][h * 128:(h + 1) * 128, :], QT, writes=[QT])
        ph.dma("sp", Kt[:], d["Ktok"][h].rearrange("(c p) f -> p c f", p=128), Kt, writes=[Kt])
        ph.dma("sp", Vt[:], d["Vtok"][h].rearrange("(c p) f -> p c f", p=128), Vt, writes=[Vt])
        ph.dma("sp", z[:], d["zT"][h * 128:(h + 1) * 128, 0:TO], z, writes=[z])
        for di in range(2):
            U = cst["Uf" if di == 0 else "Ub"]
            MS = cst["MSf" if di == 0 else "MSb"]
            MI = cst["MIf" if di == 0 else "MIb"]
            if di == 0:
                seq = [(NS + c, False) for c in range(NCC)] + [(c, True) for c in range(NO)]
            else:
                seq = [(NS + c, False) for c in reversed(range(NCC))] + [(c, False) for c in reversed(range(NO, NS))] + \
                      [(c, True) for c in reversed(range(NO))]
            S = Sr.next()
            ph.op("pool", lambda e, S=S: e.memset(S[:], 0.0), writes=[S])
            for (c, outp) in seq:
                gcol = sc[:, c, di * 16 + h: di * 16 + h + 1]
                bcol = sc[:, c, 32 + di * 16 + h: 32 + di * 16 + h + 1]
                KTc = KT[:, c * 128:(c + 1) * 128]
                p = nps()
                ph.op("pe", lambda e, p=p, U=U, gcol=gcol: e.matmul(p[:, 0:1], U[:], gcol, start=True, stop=True),
                      reads=[U, sc], writes=[p], inc=False)
                ph.op("pe", lambda e, p=p, gcol=gcol: e.matmul(p[:, 1:2], onesf[:], gcol, start=True, stop=True),
                      reads=[onesf, sc], writes=[p])
                sm = smr.next()
                ph.op("dve", lambda e, sm=sm, p=p: e.tensor_copy(out=sm[:, 0:2], in_=p[:, 0:2]), reads=[p], writes=[sm])
                ph.op("act", lambda e, sm=sm: e.activation(out=sm[:, 2:3], in_=sm[:, 0:1], func=AF.Exp), reads=[sm], writes=[sm])
                ph.op("act", lambda e, sm=sm: e.activation(out=sm[:, 3:4], in_=sm[:, 0:1], func=AF.Exp, scale=-1.0,
                                                          bias=sm[:, 1:2]), reads=[sm], writes=[sm])
                ph.op("act", lambda e, sm=sm: e.activation(out=sm[:, 4:5], in_=sm[:, 1:2], func=AF.Exp), reads=[sm], writes=[sm])
                ph.op("dve", lambda e, sm=sm, bcol=bcol: e.tensor_scalar(out=sm[:, 5:6], in0=bcol, scalar1=-1.0, scalar2=0.0,
                                                                       op0=ALU.mult, op1=ALU.add), reads=[sm, sc], writes=[sm])
                ph.op("dve", lambda e, sm=sm, bcol=bcol: e.tensor_tensor(out=sm[:, 6:7], in0=bcol, in1=sm[:, 2:3], op=ALU.mult),
                      reads=[sm, sc], writes=[sm])
                p = nps()
                mm(p, sm[:, 0:1].to_broadcast([128, 128]), ident[:], [sm, ident])
                tp = rg["tp"].next(); D = rg["D"].next()
                ph.op("dve", lambda e, tp=tp, p=p, sm=sm: e.tensor_scalar(out=tp[:], in0=p[:, :128], scalar1=sm[:, 0:1], scalar2=0.0,
                                                                        op0=ALU.subtract, op1=ALU.max), reads=[p, sm], writes=[tp])
                ph.op("act", lambda e, tp=tp, D=D: e.activation(out=D[:], in_=tp[:], func=AF.Exp, scale=-1.0), reads=[tp], writes=[D])
                ph.op("pool", lambda e, D=D, MS=MS: e.tensor_tensor(out=D[:], in0=D[:], in1=MS[:], op=ALU.mult),
                      reads=[D, MS], writes=[D])
                if outp:
                    tn = rg["tn"].next(); DT = rg["DT"].next(); grow = rg["grow"].next()
                    ph.op("dve", lambda e, tn=tn, p=p, sm=sm: e.tensor_scalar(out=tn[:], in0=p[:, :128], scalar1=sm[:, 0:1],
                                                                            scalar2=0.0, op0=ALU.subtract, op1=ALU.min),
                          reads=[p, sm], writes=[tn])
                    ph.op("act", lambda e, tn=tn, DT=DT: e.activation(out=DT[:], in_=tn[:], func=AF.Exp), reads=[tn], writes=[DT])
                    ph.op("pool", lambda e, DT=DT, MI=MI: e.tensor_tensor(out=DT[:], in0=DT[:], in1=MI[:], op=ALU.mult),
                          reads=[DT, MI], writes=[DT])
                    ph.op("act", lambda e, grow=grow, p=p: e.activation(out=grow[:], in_=p[:, :128], func=AF.Exp),
                          reads=[p], writes=[grow])
                p = nps()
                mm(p, KTc, KTc, [KT])
                B = rg["B"].next()
                ph.op("dve", lambda e, B=B, p=p, sm=sm, D=D: e.scalar_tensor_tensor(
                    out=B[:], in0=p[:, :128], scalar=sm[:, 5:6], in1=D[:], op0=ALU.mult, op1=ALU.mult),
                    reads=[p, sm, D], writes=[B])
                p = nps()
                ph.op("pe", lambda e, p=p, B=B: e.transpose(p[:, :128], B[:], ident[:]), reads=[B, ident], writes=[p])
                A = rg["A"].next()
                evac(A[:], p[:, :128], [p], [A])
                Rm = rg["Rm"].next()
                ph.op("pool", lambda e, Rm=Rm, A=A: e.tensor_tensor(out=Rm[:], in0=A[:], in1=ident[:], op=ALU.add),
                      reads=[A, ident], writes=[Rm])
                for lvl in range(6):
                    last = lvl == 5
                    pB = nps()
                    mm(pB, A[:], B[:], [A, B])
                    if not last:
                        pA = nps()
                        mm(pA, B[:], A[:], [A, B])
                    Bn = rg["B"].next()
                    evac(Bn[:], pB[:, :128], [pB], [Bn])
                    if not last:
                        An = rg["A"].next()
                        evac(An[:], pA[:, :128], [pA], [An])
                        A = An
                    B = Bn
                    pR = nps()
                    mm(pR, B[:], Rm[:], [B, Rm])
                    Rn = rg["Rm"].next()
                    ph.op("dve", lambda e, Rn=Rn, pR=pR, Rm=Rm: e.tensor_tensor(out=Rn[:], in0=pR[:, :128], in1=Rm[:], op=ALU.add),
                          reads=[pR, Rm], writes=[Rn])
                    Rm = Rn
                bv = rg["bv"].next(); kbg = rg["kbg"].next(); ktl = rg["ktl"].next()
                ph.op("pool", lambda e, bv=bv, Vt=Vt, c=c, bcol=bcol: e.tensor_scalar(out=bv[:], in0=Vt[:, c, :], scalar1=bcol,
                                                                                   scalar2=0.0, op0=ALU.mult, op1=ALU.add),
                      reads=[Vt, sc], writes=[bv])
                ph.op("pool", lambda e, kbg=kbg, Kt=Kt, c=c, sm=sm: e.tensor_scalar(out=kbg[:], in0=Kt[:, c, :], scalar1=sm[:, 6:7],
                                                                                 scalar2=0.0, op0=ALU.mult, op1=ALU.add),
                      reads=[Kt, sm], writes=[kbg])
                ph.op("pool", lambda e, ktl=ktl, Kt=Kt, c=c, sm=sm: e.tensor_scalar(out=ktl[:], in0=Kt[:, c, :], scalar1=sm[:, 3:4],
                                                                                 scalar2=0.0, op0=ALU.mult, op1=ALU.add),
                      reads=[Kt, sm], writes=[ktl])
                p = nps()
                mm(p, Rm[:], bv[:], [Rm, bv])
                u = rg["u"].next()
                evac(u[:], p[:, :128], [p], [u])
                p = nps()
                mm(p, kbg[:], Rm[:], [Rm, kbg])
                wT = rg["wT"].next()
                evac(wT[:], p[:, :128], [p], [wT])
                if outp:
                    p = nps()
                    mm(p, KTc, QT[:, c * 128:(c + 1) * 128], [KT, QT])
                    qk = rg["qk"].next()
                    ph.op("dve", lambda e, qk=qk, p=p, DT=DT: e.tensor_tensor(out=qk[:], in0=p[:, :128], in1=DT[:], op=ALU.mult),
                          reads=[p, DT], writes=[qk])
                    qd = rg["qd"].next()
                    ph.op("pool", lambda e, qd=qd, QT=QT, c=c, grow=grow: e.tensor_tensor(
                        out=qd[:], in0=QT[:, c * 128:(c + 1) * 128], in1=grow[:], op=ALU.mult), reads=[QT, grow], writes=[qd])
                p = nps()
                mm(p, wT[:], S[:], [wT, S])
                vn = rg["vn"].next()
                ph.op("dve", lambda e, vn=vn, u=u, p=p: e.tensor_tensor(out=vn[:], in0=u[:], in1=p[:, :128], op=ALU.subtract),
                      reads=[u, p], writes=[vn])
                if outp:
                    p1 = nps()
                    mm(p1, S[:], qd[:], [S, qd])
                    p2 = nps()
                    mm(p2, vn[:], qk[:], [vn, qk])
                    o1 = rg["o1"].next()
                    osl = o[:, c * 128:(c + 1) * 128]
                    ph.op("act", lambda e, o1=o1, p1=p1: e.copy(out=o1[:], in_=p1[:, :128]), reads=[p1], writes=[o1])
                    if di == 0:
                        ph.op("dve", lambda e, osl=osl, o1=o1, p2=p2: e.tensor_tensor(out=osl, in0=o1[:], in1=p2[:, :128], op=ALU.add),
                              reads=[o1, p2], writes=[o])
                    else:
                        ph.op("dve", lambda e, o1=o1, p2=p2: e.tensor_tensor(out=o1[:], in0=o1[:], in1=p2[:, :128], op=ALU.add),
                              reads=[o1, p2], writes=[o1])
                        ph.op("pool", lambda e, osl=osl, o1=o1: e.tensor_tensor(out=osl, in0=osl, in1=o1[:], op=ALU.add),
                              reads=[o1, o], writes=[o])
                p = nps()
                mm(p, ktl[:], vn[:], [ktl, vn])
                Sn = Sr.next()
                ph.op("dve", lambda e, Sn=Sn, S=S, sm=sm, p=p: e.scalar_tensor_tensor(
                    out=Sn[:], in0=S[:], scalar=sm[:, 4:5], in1=p[:, :128], op0=ALU.mult, op1=ALU.add),
                    reads=[S, sm, p], writes=[Sn])
                S = Sn
        mo = mixo.next()
        ph.op("act", lambda e, z=z: e.activation(out=z[:], in_=z[:], func=AF.Silu), reads=[z], writes=[z])
        for c0 in range(0, TO, 512):
            cs = min(512, TO - c0)
            s = sqr.next(); p = nps(); r = rsr.next()
            ph.op("act", lambda e, s=s, o=o, c0=c0, cs=cs: e.activation(out=s[:, :cs], in_=o[:, c0:c0 + cs], func=AF.Square),
                  reads=[o], writes=[s])
            ph.op("pe", lambda e, p=p, s=s, cs=cs: e.matmul(p[:, :cs], onesb[:], s[:, :cs], start=True, stop=True),
                  reads=[s, onesb], writes=[p])
            ph.op("dve", lambda e, r=r, p=p, cs=cs: e.tensor_scalar(out=r[:, :cs], in0=p[:, :cs], scalar1=1.0 / 128, scalar2=1e-6,
                                                                  op0=ALU.mult, op1=ALU.add), reads=[p], writes=[r])
            ph.op("act", lambda e, r=r, cs=cs: e.activation(out=r[:, :cs], in_=r[:, :cs], func=AF.Sqrt), reads=[r], writes=[r])
            ph.op("dve", lambda e, r=r, cs=cs: e.reciprocal(out=r[:, :cs], in_=r[:, :cs]), reads=[r], writes=[r])
            ph.op("dve", lambda e, r=r, o=o, c0=c0, cs=cs: e.scalar_tensor_tensor(
                out=r[:, :cs], in0=o[:, c0:c0 + cs], scalar=gn[:, 0:1], in1=r[:, :cs], op0=ALU.mult, op1=ALU.mult),
                reads=[o, r, gn], writes=[r])
            ph.op("pool", lambda e, mo=mo, r=r, z=z, c0=c0, cs=cs: e.tensor_tensor(
                out=mo[:, c0:c0 + cs], in0=r[:, :cs], in1=z[:, c0:c0 + cs], op=ALU.mult), reads=[r, z], writes=[mo])
        ph.dma("sp", d["mixT"][h * 128:(h + 1) * 128, 0:TO], mo[:], mo, reads=[mo])
    ph.finish()


def rope_phase(prog, name, src, dst, T, cosT, sinT, permT, nch=16, ts=512):
    ph = Phase(prog, name)
    pm = ph.tile([128, 128], BF16, "pm", dma=True)
    ph.dma("pool", pm[:], permT, pm, writes=[pm])
    ct = ph.tile([128, T], F32, "cos", dma=True)
    sn = ph.tile([128, T], F32, "sin", dma=True)
    ph.dma("sp", ct[:], cosT, ct, writes=[ct])
    ph.dma("sp", sn[:], sinT, sn, writes=[sn])
    xr = ph.ring(2, [128, T], F32, "x", dma=True)
    xb = ph.ring(2, [128, T], BF16, "xb")
    orr = ph.ring(2, [128, T], BF16, "o", dma=True)
    t1r = ph.ring(2, [128, 512], F32, "t1")
    ps = ph.psum(4)
    pi = 0
    for c in range(nch):
        x = xr.next(); b = xb.next(); o = orr.next()
        ph.dma("sp", x[:], src[c * 128:(c + 1) * 128, 0:T], x, writes=[x])
        ph.op("act", lambda e, b=b, x=x: e.copy(out=b[:], in_=x[:]), reads=[x], writes=[b])
        for c0 in range(0, T, ts):
            cs = min(ts, T - c0)
            p = ps[pi % 4]; pi += 1
            t1 = t1r.next()
            ph.op("pe", lambda e, p=p, b=b, c0=c0, cs=cs: e.matmul(p[:, :cs], pm[:], b[:, c0:c0 + cs], start=True, stop=True),
                  reads=[pm, b], writes=[p])
            ph.op("pool", lambda e, t1=t1, x=x, c0=c0, cs=cs: e.tensor_tensor(out=t1[:, :cs], in0=x[:, c0:c0 + cs],
                                                                             in1=ct[:, c0:c0 + cs], op=ALU.mult),
                  reads=[x, ct], writes=[t1])
            ph.op("dve", lambda e, t1=t1, p=p, c0=c0, cs=cs: e.tensor_tensor(out=p[:, :cs], in0=p[:, :cs], in1=sn[:, c0:c0 + cs],
                                                                           op=ALU.mult), reads=[p, sn], writes=[p])
            ph.op("dve", lambda e, t1=t1, p=p, o=o, c0=c0, cs=cs: e.tensor_tensor(out=o[:, c0:c0 + cs], in0=p[:, :cs],
                                                                                in1=t1[:, :cs], op=ALU.add),
                  reads=[p, t1], writes=[o])
        ph.dma("sp", dst[c * 128:(c + 1) * 128, 0:T], o[:], o, reads=[o])
    ph.finish()


def attn_phase(prog, name, Hd, TO, TK, d, lam_init, qs=272):
    ph = Phase(prog, name)
    NKT = TK // 128
    onesb = ph.tile([128, 128], BF16, "onesb")
    onesf = ph.tile([128, 128], F32, "onesf")
    ph.op("pool", lambda e: e.memset(onesb[:], 1.0), writes=[onesb])
    ph.op("pool", lambda e: e.memset(onesf[:], 1.0), writes=[onesf])
    lv = ph.tile([128, 4], F32, "lv", dma=True)
    dn = ph.tile([128, 2], F32, "dn", dma=True)
    ph.dma("sp", lv[:], d["lamv"], lv, writes=[lv])
    ph.dma("sp", dn[:], d["dnorm"], dn, writes=[dn])
    ps = ph.psum(8)
    pr = ph.tile([128, 2], F32, "pr")
    nl = ph.tile([128, 2], F32, "nl")
    ph.op("dve", lambda e: e.tensor_tensor(out=pr[:, 0:1], in0=lv[:, 0:1], in1=lv[:, 1:2], op=ALU.mult), reads=[lv], writes=[pr])
    ph.op("dve", lambda e: e.tensor_tensor(out=pr[:, 1:2], in0=lv[:, 2:3], in1=lv[:, 3:4], op=ALU.mult), reads=[lv, pr], writes=[pr])
    ph.op("pe", lambda e: e.matmul(ps[7][:, 0:2], onesf[:], pr[:], start=True, stop=True), reads=[pr, onesf], writes=[ps[7]])
    ph.op("act", lambda e: e.activation(out=nl[:], in_=ps[7][:, 0:2], func=AF.Exp), reads=[ps[7]], writes=[nl])
    ph.op("dve", lambda e: e.tensor_tensor(out=nl[:, 0:1], in0=nl[:, 1:2], in1=nl[:, 0:1], op=ALU.subtract), reads=[nl], writes=[nl])
    ph.op("dve", lambda e: e.tensor_scalar(out=nl[:, 0:1], in0=nl[:, 0:1], scalar1=-float(lam_init), scalar2=1.0, op0=ALU.add,
                                           op1=ALU.mult), reads=[nl], writes=[nl])
    ph.op("dve", lambda e: e.tensor_scalar(out=dn[:], in0=dn[:], scalar1=1.0 - float(lam_init), scalar2=0.0, op0=ALU.mult,
                                           op1=ALU.add), reads=[dn], writes=[dn])
    Kr = ph.ring(2, [128, 2, TK], BF16, "K", dma=True)
    Qr = ph.ring(2, [128, 2, TO], BF16, "Q", dma=True)
    Vr = ph.ring(2, [128, NKT, 256], BF16, "V", dma=True)
    Er = ph.ring(3, [128, qs], BF16, "E")
    Or = ph.ring(2, [128, 4, qs], F32, "O")
    rz = ph.ring(2, [128, qs], F32, "rz")
    dfr = ph.ring(2, [128, 2, qs], F32, "df")
    sqr = ph.ring(2, [128, qs], BF16, "sq")
    outr = ph.ring(2, [128, 2, TO], BF16, "out", dma=True)
    sbank = [ps[0], ps[1]]
    for h in range(Hd):
        K = Kr.next(); Q = Qr.next(); V = Vr.next(); ot = outr.next()
        ph.dma("sp", K[:], d["KrT"][h * 256:(h + 1) * 256, :].rearrange("(m p) t -> p m t", p=128), K, writes=[K])
        ph.dma("sp", Q[:], d["QrT"][h * 256:(h + 1) * 256, 0:TO].rearrange("(m p) t -> p m t", p=128), Q, writes=[Q])
        ph.dma("sp", V[:], d["Vtok"][:, h * 256:(h + 1) * 256].rearrange("(c p) e -> p c e", p=128), V, writes=[V])
        for q0 in range(0, TO, qs):
            O = Or.next()
            si = 0
            for m in range(2):
                acc = [ps[2 + 3 * m], ps[3 + 3 * m], ps[4 + 3 * m]]
                for kt in range(NKT):
                    sp_ = sbank[si % 2]; si += 1
                    ph.op("pe", lambda e, sp_=sp_, K=K, Q=Q, m=m, kt=kt, q0=q0: e.matmul(
                        sp_[:, :qs], K[:, m, kt * 128:(kt + 1) * 128], Q[:, m, q0:q0 + qs], start=True, stop=True),
                        reads=[K, Q], writes=[sp_])
                    E = Er.next()
                    ph.op("act", lambda e, E=E, sp_=sp_: e.activation(out=E[:], in_=sp_[:, :qs], func=AF.Exp), reads=[sp_], writes=[E])
                    last = kt == NKT - 1
                    for j in range(2):
                        ph.op("pe", lambda e, a=acc[j], V=V, kt=kt, j=j, E=E: e.matmul(
                            a[:, :qs], V[:, kt, j * 128:(j + 1) * 128], E[:], start=(kt == 0), stop=(kt == NKT - 1)),
                            reads=[V, E], writes=[acc[j]], inc=last)
                    ph.op("pe", lambda e, a=acc[2], kt=kt, E=E: e.matmul(a[:, :qs], onesb[:], E[:], start=(kt == 0), stop=(kt == NKT - 1)),
                          reads=[onesb, E], writes=[acc[2]], inc=True)
                r = rz.next()
                ph.op("dve", lambda e, r=r, a=acc[2]: e.reciprocal(out=r[:], in_=a[:, :qs]), reads=[acc[2]], writes=[r])
                for j in range(2):
                    ph.op("dve", lambda e, O=O, a=acc[j], r=r, m=m, j=j: e.tensor_tensor(out=O[:, m * 2 + j, :], in0=a[:, :qs], in1=r[:],
                                                                                     op=ALU.mult), reads=[acc[j], r], writes=[O])
            df = dfr.next()
            for j in range(2):
                ph.op("dve", lambda e, df=df, O=O, j=j: e.scalar_tensor_tensor(out=df[:, j, :], in0=O[:, 2 + j, :], scalar=nl[:, 0:1],
                                                                              in1=O[:, j, :], op0=ALU.mult, op1=ALU.add),
                      reads=[O, nl], writes=[df])
            pz = ps[0] if (si % 2 == 0) else ps[1]
            for j in range(2):
                s = sqr.next()
                ph.op("act", lambda e, s=s, df=df, j=j: e.activation(out=s[:], in_=df[:, j, :], func=AF.Square), reads=[df], writes=[s])
                ph.op("pe", lambda e, pz=pz, s=s, j=j: e.matmul(pz[:, :qs], onesb[:], s[:], start=(j == 0), stop=(j == 1)),
                      reads=[s, onesb], writes=[pz], inc=(j == 1))
            si += 1
            r = rz.next()
            ph.op("dve", lambda e, r=r, pz=pz: e.tensor_scalar(out=r[:], in0=pz[:, :qs], scalar1=1.0 / 256, scalar2=1e-5, op0=ALU.mult,
                                                             op1=ALU.add), reads=[pz], writes=[r])
            ph.op("act", lambda e, r=r: e.activation(out=r[:], in_=r[:], func=AF.Sqrt), reads=[r], writes=[r])
            ph.op("dve", lambda e, r=r: e.reciprocal(out=r[:], in_=r[:]), reads=[r], writes=[r])
            for j in range(2):
                ph.op("dve", lambda e, ot=ot, df=df, r=r, j=j, q0=q0: e.scalar_tensor_tensor(
                    out=ot[:, j, q0:q0 + qs], in0=df[:, j, :], scalar=dn[:, j:j + 1], in1=r[:], op0=ALU.mult, op1=ALU.mult),
                    reads=[df, dn, r], writes=[ot])
        ph.dma("sp", d["mixT"][d["mixoff"] + h * 256: d["mixoff"] + (h + 1) * 256, 0:TO].rearrange("(j p) t -> p j t", p=128),
               ot[:], ot, reads=[ot])
    ph.finish()


def conf_phase(prog, name, TO, d, NCH=16, TB=1088, KW=31):
    ph = Phase(prog, name)
    PAD = (KW - 1) // 2
    NC_ALL = NCH * 128
    dw = ph.tile([128, NCH, KW], F32, "dw", dma=True)
    vb = ph.tile([128, NCH, 3], F32, "vb", dma=True)
    ph.dma("sp", dw[:], d["conf_dw"], dw, writes=[dw])
    ph.dma("sp", vb[:], d["conf_vec"], vb, writes=[vb])
    onesb = ph.tile([128, 128], BF16, "onesb")
    ph.op("pool", lambda e: e.memset(onesb[:], 1.0), writes=[onesb])
    hold = ph.tile([128, NCH, TB], F32, "hold")
    ar = ph.ring(2, [128, TB + 2 * PAD], F32, "a", dma=True)
    br = ph.ring(2, [128, TB + 2 * PAD], F32, "b", dma=True)
    a0r = ph.ring(2, [128, TB], F32, "acc0")
    a1r = ph.ring(2, [128, TB], F32, "acc1")
    tmr = ph.ring(2, [128, TB], F32, "tmp")
    ybr = ph.ring(2, [128, TB], BF16, "yb")
    sqr = ph.ring(2, [128, TB], BF16, "sq")
    mean = ph.tile([128, TB], F32, "mean")
    rstd = ph.tile([128, TB], F32, "rstd")
    outr = ph.ring(2, [128, TB], BF16, "out", dma=True)
    ps = ph.psum(8)
    pieces = [(c0, min(512, TB - c0)) for c0 in range(0, TB, 512)]
    assert len(pieces) <= 3
    for tb in range(TO // TB):
        t0 = tb * TB
        lo, hi = t0 - PAD, t0 + TB + PAD
        vlo, vhi = max(lo, 0), min(hi, TO)
        for c in range(NCH):
            a = ar.next(); b = br.next()
            for (t, src) in ((a, "glu_a"), (b, "glu_b")):
                if vlo > lo:
                    ph.op("pool", lambda e, t=t, n=vlo - lo: e.memset(t[:, 0:n], 0.0), writes=[t])
                if vhi < hi:
                    ph.op("pool", lambda e, t=t, n0=vhi - lo, n1=hi - lo: e.memset(t[:, n0:n1], 0.0), writes=[t])
                ph.dma("sp", t[:, vlo - lo:vhi - lo], d[src][c * 128:(c + 1) * 128, vlo:vhi], t, writes=[t])
            ph.op("act", lambda e, b=b: e.activation(out=b[:], in_=b[:], func=AF.Sigmoid), reads=[b], writes=[b])
            ph.op("pool", lambda e, a=a, b=b: e.tensor_tensor(out=a[:], in0=a[:], in1=b[:], op=ALU.mult), reads=[a, b], writes=[a])
            acc0 = a0r.next(); acc1 = a1r.next()
            KH = 16
            for k in range(KH):
                if k == 0:
                    ph.op("dve", lambda e, acc0=acc0, a=a, c=c: e.tensor_scalar(out=acc0[:], in0=a[:, 0:TB], scalar1=dw[:, c, 0:1],
                                                                               scalar2=0.0, op0=ALU.mult, op1=ALU.add),
                          reads=[a, dw], writes=[acc0])
                else:
                    ph.op("dve", lambda e, acc0=acc0, a=a, c=c, k=k: e.scalar_tensor_tensor(
                        out=acc0[:], in0=a[:, k:k + TB], scalar=dw[:, c, k:k + 1], in1=acc0[:], op0=ALU.mult, op1=ALU.add),
                        reads=[a, dw, acc0], writes=[acc0])
            for k in range(KH, KW):
                if k == KH:
                    ph.op("act", lambda e, acc1=acc1, a=a, c=c, k=k: e.activation(out=acc1[:], in_=a[:, k:k + TB], func=AF.Copy,
                                                                                scale=dw[:, c, k:k + 1]), reads=[a, dw], writes=[acc1])
                else:
                    tm = tmr.next()
                    ph.op("act", lambda e, tm=tm, a=a, c=c, k=k: e.activation(out=tm[:], in_=a[:, k:k + TB], func=AF.Copy,
                                                                            scale=dw[:, c, k:k + 1]), reads=[a, dw], writes=[tm])
                    ph.op("pool", lambda e, tm=tm, acc1=acc1: e.tensor_tensor(out=acc1[:], in0=acc1[:], in1=tm[:], op=ALU.add),
                          reads=[tm, acc1], writes=[acc1])
            ph.op("dve", lambda e, acc0=acc0, acc1=acc1, c=c: e.scalar_tensor_tensor(
                out=hold[:, c, :], in0=acc1[:], scalar=vb[:, c, 0:1], in1=acc0[:], op0=ALU.add, op1=ALU.add),
                reads=[acc0, acc1, vb], writes=[hold])
            yb = ybr.next(); sq = sqr.next()
            ph.op("act", lambda e, yb=yb, c=c: e.copy(out=yb[:], in_=hold[:, c, :]), reads=[hold], writes=[yb])
            ph.op("act", lambda e, sq=sq, c=c: e.activation(out=sq[:], in_=hold[:, c, :], func=AF.Square), reads=[hold], writes=[sq])
            for pi_, (c0, cs) in enumerate(pieces):
                ph.op("pe", lambda e, pi_=pi_, yb=yb, c=c, c0=c0, cs=cs: e.matmul(ps[pi_][:, :cs], onesb[:], yb[:, c0:c0 + cs],
                                                                                 start=(c == 0), stop=(c == NCH - 1)),
                      reads=[yb, onesb], writes=[ps[pi_]], inc=True)
                ph.op("pe", lambda e, pi_=pi_, sq=sq, c=c, c0=c0, cs=cs: e.matmul(ps[3 + pi_][:, :cs], onesb[:], sq[:, c0:c0 + cs],
                                                                                 start=(c == 0), stop=(c == NCH - 1)),
                      reads=[sq, onesb], writes=[ps[3 + pi_]], inc=True)
        for pi_, (c0, cs) in enumerate(pieces):
            ph.op("dve", lambda e, pi_=pi_, c0=c0, cs=cs: e.tensor_scalar(out=mean[:, c0:c0 + cs], in0=ps[pi_][:, :cs],
                                                                        scalar1=1.0 / NC_ALL, scalar2=0.0, op0=ALU.mult, op1=ALU.add),
                  reads=[ps[pi_]], writes=[mean])
            ph.op("dve", lambda e, pi_=pi_, c0=c0, cs=cs: e.tensor_scalar(out=rstd[:, c0:c0 + cs], in0=ps[3 + pi_][:, :cs],
                                                                        scalar1=1.0 / NC_ALL, scalar2=1e-5, op0=ALU.mult, op1=ALU.add),
                  reads=[ps[3 + pi_]], writes=[rstd])
        tm = tmr.next()
        ph.op("pool", lambda e, tm=tm: e.tensor_tensor(out=tm[:], in0=mean[:], in1=mean[:], op=ALU.mult), reads=[mean], writes=[tm])
        ph.op("pool", lambda e, tm=tm: e.tensor_tensor(out=rstd[:], in0=rstd[:], in1=tm[:], op=ALU.subtract), reads=[rstd, tm], writes=[rstd])
        ph.op("act", lambda e: e.activation(out=rstd[:], in_=rstd[:], func=AF.Sqrt), reads=[rstd], writes=[rstd])
        ph.op("dve", lambda e: e.reciprocal(out=rstd[:], in_=rstd[:]), reads=[rstd], writes=[rstd])
        for c in range(NCH):
            tm = tmr.next(); o = outr.next()
            ph.op("pool", lambda e, tm=tm, c=c: e.tensor_tensor(out=tm[:], in0=hold[:, c, :], in1=mean[:], op=ALU.subtract),
                  reads=[hold, mean], writes=[tm])
            ph.op("dve", lambda e, tm=tm: e.tensor_tensor(out=tm[:], in0=tm[:], in1=rstd[:], op=ALU.mult), reads=[tm, rstd], writes=[tm])
            ph.op("act", lambda e, tm=tm, o=o, c=c: e.activation(out=o[:], in_=tm[:], func=AF.Silu, scale=vb[:, c, 1:2],
                                                               bias=vb[:, c, 2:3]), reads=[tm, vb], writes=[o])
            ph.dma("sp", d["mixT"][c * 128:(c + 1) * 128, t0:t0 + TB], o[:], o, reads=[o])
    ph.finish()


def sc_phase(prog, name, TO, d, NCH=16, off=2048):
    ph = Phase(prog, name)
    cw = ph.tile([128, NCH, 3], F32, "cw", dma=True)
    ph.dma("sp", cw[:], d["sc_w"], cw, writes=[cw])
    gcr = ph.ring(2, [128, TO + 2], F32, "gc", dma=True)
    shr = ph.ring(2, [128, TO], F32, "sh", dma=True)
    gbr = ph.ring(2, [128, TO], F32, "gb", dma=True)
    acr = ph.ring(2, [128, TO], F32, "acc")
    outr = ph.ring(2, [128, TO], BF16, "out", dma=True)
    for c in range(NCH):
        gc = gcr.next(); sh = shr.next(); gb = gbr.next(); acc = acr.next(); o = outr.next()
        ph.op("pool", lambda e, gc=gc: e.memset(gc[:, 0:1], 0.0), writes=[gc])
        ph.op("pool", lambda e, gc=gc: e.memset(gc[:, TO + 1:TO + 2], 0.0), writes=[gc])
        ph.dma("sp", gc[:, 1:TO + 1], d["gate_c"][c * 128:(c + 1) * 128, 0:TO], gc, writes=[gc])
        ph.dma("sp", sh[:], d["sh"][c * 128:(c + 1) * 128, 0:TO], sh, writes=[sh])
        ph.dma("sp", gb[:], d["gate_b"][c * 128:(c + 1) * 128, 0:TO], gb, writes=[gb])
        ph.op("pool", lambda e, gc=gc, sh=sh: e.tensor_tensor(out=gc[:, 1:TO + 1], in0=gc[:, 1:TO + 1], in1=sh[:], op=ALU.mult),
              reads=[gc, sh], writes=[gc])
        ph.op("act", lambda e, acc=acc, gc=gc, c=c: e.activation(out=acc[:], in_=gc[:, 0:TO], func=AF.Copy, scale=cw[:, c, 0:1]),
              reads=[gc, cw], writes=[acc])
        for k in (1, 2):
            ph.op("dve", lambda e, acc=acc, gc=gc, c=c, k=k: e.scalar_tensor_tensor(
                out=acc[:], in0=gc[:, k:k + TO], scalar=cw[:, c, k:k + 1], in1=acc[:], op0=ALU.mult, op1=ALU.add),
                reads=[gc, cw, acc], writes=[acc])
        ph.op("pool", lambda e, o=o, acc=acc, gb=gb: e.tensor_tensor(out=o[:], in0=acc[:], in1=gb[:], op=ALU.mult),
              reads=[acc, gb], writes=[o])
        ph.dma("sp", d["mixT"][off + c * 128: off + (c + 1) * 128, 0:TO], o[:], o, reads=[o])
    ph.finish()
import math
from concourse.bass_utils import run_bass_kernel_spmd

D = 4096; TO = 2176; TOTH = 1920; TC = 256; TS = 4096; TK = TS + TC; DFF = 11008
SUBS_OWN = [(i * 272, 272) for i in range(8)]
SUBS_P2 = [(i * 480, 480) for i in range(4)] + [(1920, 256)]
_ev = [0]


def _evac(ph, dst, src, reads, writes, func=None):
    _ev[0] += 1
    if func is not None or _ev[0] % 2:
        f = func or AF.Copy
        ph.op("act", lambda e: e.activation(out=dst, in_=src, func=f), reads=reads, writes=writes)
    else:
        ph.op("dve", lambda e: e.tensor_copy(out=dst, in_=src), reads=reads, writes=writes)


def fm_store_epi(dst_of, T, dtype=F32, acc=False):
    state = {}

    def extra(ph):
        if acc:
            return (ph.ring(3, [128, T], dtype, "stg", dma=True), ph.ring(2, [128, T], dtype, "prev", dma=True))
        return ph.ring(3, [128, T], dtype, "stg", dma=True)

    def epi(ph, st, n0, j, si, p, c0, cs, nsub=None):
        ring = st[0] if acc else st
        if si == 0:
            state["t"] = ring.next()
            if acc:
                pv = st[1].next()
                state["pv"] = pv
                dst, r0, col0 = dst_of(n0, j)
                ph.dma("sp", pv[:], dst[r0:r0 + 128, col0:col0 + T], pv, writes=[pv])
        t = state["t"]
        _evac(ph, t[:, c0:c0 + cs], p[:, :cs], [p], [t])
        if c0 + cs == T:
            dst, r0, col0 = dst_of(n0, j)
            if acc:
                pv = state["pv"]
                ph.op("pool", lambda e: e.tensor_tensor(out=t[:], in0=t[:], in1=pv[:], op=ALU.add), reads=[t, pv], writes=[t])
            ph.dma("sp", dst[r0:r0 + 128, col0:col0 + T], t[:], t, reads=[t])
    return extra, epi


def ffn_phases(prog, I, V, l, xin, xout, tag):
    norm_phase(prog, tag + "pre", xin, TO, I["xnT"], V(l, 3), bvec=V(l, 4))
    st8 = {}

    def extra(ph):
        return (ph.ring(2, [128, 1, TO], F32, "sg"), ph.ring(3, [128, TO], BF16, "hst", dma=True))

    def epi_g(ph, st, key, j, si, p, c0, cs):
        if j == 0 and si == 0:
            st8["sg"] = st[0].next()
        sg = st8["sg"]
        ph.op("act", lambda e: e.activation(out=sg[:, j, c0:c0 + cs], in_=p[:, :cs], func=AF.Silu), reads=[p], writes=[sg])

    def epi_u(ph, st, key, j, si, p, c0, cs):
        if si == 0:
            st8["h"] = st[1].next()
        sg = st8["sg"]; h = st8["h"]
        ph.op("dve", lambda e: e.tensor_tensor(out=h[:, c0:c0 + cs], in0=sg[:, j, c0:c0 + cs], in1=p[:, :cs], op=ALU.mult),
              reads=[sg, p], writes=[h])
        if c0 + cs == TO:
            r0 = key[1] + j * 128
            ph.dma("sp", I["hT"][r0:r0 + 128, 0:TO], h[:], h, reads=[h])
    blocks = []
    for n0 in range(0, DFF, 128):
        blocks.append((("g", n0), 128, "fm", epi_g))
        blocks.append((("u", n0), 128, "fm", epi_u))
    wsel = lambda key, nb: (I["w_ff_gate"] if key[0] == "g" else I["w_ff_up"])[l][:, key[1]:key[1] + nb]
    gemm_phase(prog, tag + "f1", I["xnT"], 0, TO, D, wsel, blocks, SUBS_OWN, extra=extra, NBW=128)
    for si_, (k0, k1) in enumerate(((0, 4096), (4096, 8192), (8192, DFF))):
        ex, ep = fm_store_epi(lambda n0, j: (I["yT"], n0 + j * 128, 0), TO, acc=(si_ > 0))
        gemm_phase(prog, tag + "f2_%d" % si_, I["hT"][k0:k1, :], 0, TO, k1 - k0,
                   lambda n0, nb, k0=k0, k1=k1: I["w_ff_down"][l][k0:k1, n0:n0 + nb],
                   [(n0, 128, "fm", ep) for n0 in range(0, D, 128)], SUBS_OWN, extra=ex, NBW=128)
    norm_phase(prog, tag + "post", I["yT"], TO, xout, V(l, 5), resid=xin)


def build_program(max_phases=None, debug_outs=(), min_phase=0):
    nc = bass.Bass("TRN2", target_bir_lowering=False)
    I = {}

    def inp(nm, shape, dt=F32):
        I[nm] = nc.dram_tensor(nm, list(shape), dt, kind="ExternalInput").ap()

    def scr(nm, shape, dt=F32):
        if nm in debug_outs:
            I[nm] = nc.dram_tensor(nm, list(shape), dt, kind="ExternalOutput").ap()
        else:
            I[nm] = nc.dram_tensor(nm, list(shape), dt).ap()

    inp("xT_own", [D, TO]); inp("xT_oth", [D, TOTH]); inp("ctxT", [D, TC]); inp("cvec", [128, 32, 2])
    inp("w_ada", [2, D, 6 * D]); inp("bada", [2, 128, 192]); inp("gains", [2, 4, 128, 32])
    inp("w_ff_gate", [2, D, DFF]); inp("w_ff_up", [2, D, DFF]); inp("w_ff_down", [2, DFF, D])
    inp("w_in_even", [D, 14400]); inp("w_ab", [D, 64]); inp("w_out_even", [D, D])
    inp("w_in_odd", [D, 10240]); inp("w_out_odd", [D, D])
    inp("convw", [128, 144]); inp("alog", [128, 32]); inp("dtb", [128, 32]); inp("gnorm", [128, 1])
    inp("lamv", [128, 4]); inp("dnorm", [128, 2])
    inp("conf_dw", [128, 16, 31]); inp("conf_vec", [128, 16, 3]); inp("sc_w", [128, 16, 3])
    inp("cosq", [128, TO]); inp("sinq", [128, TO]); inp("cosk", [128, TK]); inp("sink", [128, TK]); inp("permT", [128, 128])
    for nm in ("ident", "Uf", "Ub", "MSf", "MSb", "MIf", "MIb"):
        inp(nm, [128, 128])
    outT = nc.dram_tensor("outT", [D, TO], F32, kind="ExternalOutput").ap()
    scr("scT", [D, 2], BF16); scr("modsD", [2, 128, 192, 2]); scr("vecsD", [2, 8, 128, 32])
    scr("xnT", [D, TO], BF16); scr("xnT2", [D, TO], BF16)
    scr("gq_pre", [2048, TO]); scr("gk_pre", [2048, TS]); scr("gv_pre", [2048, TS]); scr("ck_pre", [2048, TC]); scr("cv_pre", [2048, TC])
    scr("zT", [2048, TO]); scr("dq_pre", [2048, TO]); scr("dk_pre", [2048, TK]); scr("Vtok", [TK, 2048], BF16)
    scr("ab_tok", [TS, 64]); scr("abc_tok", [TC, 64])
    scr("KT", [2048, TK]); scr("QT", [2048, TO]); scr("Ktok", [16, TK, 128]); scr("Vtokg", [16, TK, 128]); scr("sc_tok", [TK, 64])
    scr("QrT", [2048, TO], BF16); scr("KrT", [2048, TK], BF16); scr("mixT", [D, TO], BF16)
    scr("yT", [D, TO]); scr("x1T", [D, TO]); scr("x2T", [D, TO]); scr("x3T", [D, TO]); scr("hT", [DFF, TO], BF16)
    for nm in ("glu_a", "glu_b", "gate_b", "gate_c", "sh"):
        scr(nm, [2048, TO])

    with ExitStack() as es:
        prog = Prog(nc, es)
        prog.max_phases = max_phases
        prog.min_phase = min_phase
        ph = Phase(prog, "silu")
        cv = ph.tile([128, 32, 2], F32, "cv", dma=True)
        cb = ph.tile([128, 32, 2], BF16, "cb", dma=True)
        ph.dma("sp", cv[:], I["cvec"], cv, writes=[cv])
        ph.op("act", lambda e: e.activation(out=cb[:], in_=cv[:], func=AF.Silu), reads=[cv], writes=[cb])
        ph.dma("sp", I["scT"].rearrange("(j p) m -> p j m", p=128), cb[:], cb, reads=[cb])
        ph.finish()
        for l in range(2):
            def extra(ph):
                return ph.tile([128, 192, 2], F32, "mods", dma=True)

            def epi(ph, st, n0, j, si, p, c0, cs):
                nt = n0 // 128 + j
                _evac(ph, st[:, nt, :], p[:, 0:2], [p], [st])

            def post(ph, st, l=l):
                ph.dma("sp", I["modsD"][l], st[:], st, reads=[st])
            gemm_phase(prog, "ada%d" % l, I["scT"], 0, 2, D, lambda n0, nb, l=l: I["w_ada"][l][:, n0:n0 + nb],
                       [(n0, 256, "fm", epi) for n0 in range(0, 6 * D, 256)], [(0, 2)], extra=extra, post=post)
        ph = Phase(prog, "vecs")
        def do_layer(l):
            md = ph.tile([128, 192, 2], F32, "md", dma=True)
            bd = ph.tile([128, 192], F32, "bd", dma=True)
            gt = ph.tile([128, 4, 32], F32, "gt", dma=True)
            vt = ph.tile([128, 8, 32], F32, "vt", dma=True)
            ph.dma("sp", md[:], I["modsD"][l], md, writes=[md])
            ph.dma("sp", bd[:], I["bada"][l], bd, writes=[bd])
            ph.dma("sp", gt[:], I["gains"][l].rearrange("g p j -> p g j"), gt, writes=[gt])
            mv = lambda v, m: md[:, v * 32:(v + 1) * 32, m]
            bv = lambda v: bd[:, v * 32:(v + 1) * 32]
            def amul(k, v, g, m=0):
                ph.op("dve", lambda e: e.tensor_tensor(out=vt[:, k, :], in0=mv(v, m), in1=bv(v), op=ALU.add), reads=[md, bd], writes=[vt])
                ph.op("dve", lambda e: e.scalar_tensor_tensor(out=vt[:, k, :], in0=vt[:, k, :], scalar=1.0, in1=gt[:, g, :],
                                                              op0=ALU.add, op1=ALU.mult), reads=[vt, gt], writes=[vt])

            def badd(k, v, m=0):
                ph.op("dve", lambda e: e.tensor_tensor(out=vt[:, k, :], in0=mv(v, m), in1=bv(v), op=ALU.add), reads=[md, bd], writes=[vt])

            def cmul(k, v, g):
                ph.op("dve", lambda e: e.tensor_tensor(out=vt[:, k, :], in0=mv(v, 0), in1=bv(v), op=ALU.add), reads=[md, bd], writes=[vt])
                ph.op("dve", lambda e: e.tensor_tensor(out=vt[:, k, :], in0=vt[:, k, :], in1=gt[:, g, :], op=ALU.mult),
                      reads=[vt, gt], writes=[vt])
            amul(0, 1, 0); badd(1, 0); cmul(2, 2, 1); amul(3, 4, 2); badd(4, 3); cmul(5, 5, 3); amul(6, 1, 0, m=1); badd(7, 0, m=1)
            ph.dma("sp", I["vecsD"][l].rearrange("k p j -> p k j"), vt[:], vt, reads=[vt])
        do_layer(0)
        do_layer(1)
        ph.finish()
        V = lambda l, k: I["vecsD"][l, k]

        norm_phase(prog, "n0a", I["xT_own"], TO, I["xnT"], V(0, 0), bvec=V(0, 1))
        norm_phase(prog, "n0b", I["xT_oth"], TOTH, I["xnT2"], V(0, 0), bvec=V(0, 1), ts=240)
        norm_phase(prog, "n0c", I["ctxT"], TC, I["xnT2"], V(0, 6), bvec=V(0, 7), ts=256, ocol0=TOTH)
        def dst1(n0, j):
            n = n0 + j * 128
            if n < 2048: return (I["gq_pre"], n, 0)
            if n < 4096: return (I["gk_pre"], n - 2048, 0)
            if n < 6144: return (I["gv_pre"], n - 4096, 0)
            if n < 8192: return (I["zT"], n - 6144, 0)
            if n < 10304: return (I["dq_pre"], n - 8256, 0)
            return (I["dk_pre"], n - 10304, 0)
        ex1, ep1 = fm_store_epi(dst1, TO)
        tmst = {}

        def extra1(ph):
            return (ex1(ph), ph.ring(2, [128, 64], F32, "abst", dma=True), ph.ring(2, [128, 256], BF16, "vst", dma=True))

        def ep1w(ph, st, n0, j, si, p, c0, cs):
            ep1(ph, st[0], n0, j, si, p, c0, cs)

        def tm_epi(tokdst):
            def epi(ph, st, n0, ti, p, nb):
                if n0 == "ab":
                    t = st[1].next()
                    _evac(ph, t[:, :64], p[:, :64], [p], [t])
                    dst, r0 = tokdst(ti, True)
                    ph.dma("sp", dst[r0:r0 + 128, :], t[:, :64], t, reads=[t])
                else:
                    t = st[2].next()
                    _evac(ph, t[:, :256], p[:, :256], [p], [t])
                    dst, r0 = tokdst(ti, False)
                    ph.dma("sp", dst[r0:r0 + 128, n0 - 12352:n0 - 12352 + 256], t[:, :256], t, reads=[t])
            return epi
        wsel_in = lambda n0, nb: I["w_ab"] if n0 == "ab" else I["w_in_even"][:, n0:n0 + nb]
        te1 = tm_epi(lambda ti, ab: ((I["ab_tok"], ti * 128) if ab else (I["Vtok"], ti * 128)))
        fmcols = list(range(0, 8192, 256)) + list(range(8256, 12352, 256))
        blocks = [(n0, 256, "fm", ep1w) for n0 in fmcols] + [("ab", 64, "tm", te1)] + [(n0, 256, "tm", te1) for n0 in range(12352, 14400, 256)]
        gemm_phase(prog, "inp1", I["xnT"], 0, TO, D, wsel_in, blocks, SUBS_OWN, extra=extra1)
        def extra2(ph):
            return (ph.ring(3, [128, 480], F32, "stg2", dma=True), ph.ring(2, [128, 64], F32, "abst", dma=True),
                    ph.ring(2, [128, 256], BF16, "vst", dma=True))

        def ep2(ph, st, n0, j, si, p, c0, cs):
            t = st[0].next()
            _evac(ph, t[:, :cs], p[:, :cs], [p], [t])
            n = n0 + j * 128
            if n < 4096: nm, cnm, r = "gk_pre", "ck_pre", n - 2048
            elif n < 6144: nm, cnm, r = "gv_pre", "cv_pre", n - 4096
            else: nm, cnm, r = "dk_pre", "dk_pre", n - 10304
            if c0 < TOTH:
                ph.dma("sp", I[nm][r:r + 128, TO + c0:TO + c0 + cs], t[:, :cs], t, reads=[t])
            elif nm == "dk_pre":
                ph.dma("sp", I[nm][r:r + 128, TS:TS + cs], t[:, :cs], t, reads=[t])
            else:
                ph.dma("sp", I[cnm][r:r + 128, 0:cs], t[:, :cs], t, reads=[t])
        te2 = tm_epi(lambda ti, ab: ((I["ab_tok"], TO + ti * 128) if ti < 15 else (I["abc_tok"], (ti - 15) * 128)) if ab
                     else ((I["Vtok"], TO + ti * 128) if ti < 15 else (I["Vtok"], TS + (ti - 15) * 128)))
        fm2 = list(range(2048, 6144, 256)) + list(range(10304, 12352, 256))
        blocks = [(n0, 256, "fm", ep2) for n0 in fm2] + [("ab", 64, "tm", te2)] + [(n0, 256, "tm", te2) for n0 in range(12352, 14400, 256)]
        gemm_phase(prog, "inp2", I["xnT2"], 0, TO, D, wsel_in, blocks, SUBS_P2, extra=extra2)
        rope_phase(prog, "rq", I["dq_pre"], I["QrT"], TO, I["cosq"], I["sinq"], I["permT"], nch=16)
        rope_phase(prog, "rk", I["dk_pre"], I["KrT"], TK, I["cosk"], I["sink"], I["permT"], nch=16)
        gd = dict(I); gd["Vtok"] = I["Vtokg"]
        gdn_prep(prog, "gp", 16, TS, TO, TC, gd)
        gdn_scan(prog, "gs", 16, TS, TO, TC, gd)
        ad = dict(I); ad["mixoff"] = 2048
        attn_phase(prog, "at", 8, TO, TK, ad, 0.8 - 0.6 * math.exp(-0.3 * 0))
        ex, ep = fm_store_epi(lambda n0, j: (I["yT"], n0 + j * 128, 0), TO)
        gemm_phase(prog, "wo0", I["mixT"], 0, TO, D, lambda n0, nb: I["w_out_even"][:, n0:n0 + nb],
                   [(n0, 256, "fm", ep) for n0 in range(0, D, 256)], SUBS_OWN, extra=ex)
        norm_phase(prog, "p0", I["yT"], TO, I["x1T"], V(0, 2), resid=I["xT_own"])
        ffn_phases(prog, I, V, 0, I["x1T"], I["x2T"], "f0")
        norm_phase(prog, "n1a", I["x2T"], TO, I["xnT"], V(1, 0), bvec=V(1, 1))
        names = ["glu_a", "glu_b", "gate_b", "gate_c", "sh"]
        ex, ep = fm_store_epi(lambda n0, j: (I[names[(n0 + j * 128) // 2048]], (n0 + j * 128) % 2048, 0), TO)
        gemm_phase(prog, "inp3", I["xnT"], 0, TO, D, lambda n0, nb: I["w_in_odd"][:, n0:n0 + nb],
                   [(n0, 256, "fm", ep) for n0 in range(0, 10240, 256)], SUBS_OWN, extra=ex)
        conf_phase(prog, "cf", TO, I)
        sc_phase(prog, "scv", TO, I)
        ex, ep = fm_store_epi(lambda n0, j: (I["yT"], n0 + j * 128, 0), TO)
        gemm_phase(prog, "wo1", I["mixT"], 0, TO, D, lambda n0, nb: I["w_out_odd"][:, n0:n0 + nb],
                   [(n0, 256, "fm", ep) for n0 in range(0, D, 256)], SUBS_OWN, extra=ex)
        norm_phase(prog, "p1", I["yT"], TO, I["x3T"], V(1, 2), resid=I["x2T"])
        ffn_phases(prog, I, V, 1, I["x3T"], outT, "f1")
        print("n_inst", prog.n_inst, "phases", len(prog.phase_names), flush=True)
    return nc


def _consts():
    i = np.arange(128)
    k, p = np.meshgrid(i, i, indexing="ij")
    c = {}
    c["ident"] = np.eye(128, dtype=np.float32)
    c["Uf"] = (k <= p).astype(np.float32)
    c["Ub"] = (k >= p).astype(np.float32)
    c["MSf"] = (k > p).astype(np.float32)
    c["MSb"] = (k < p).astype(np.float32)
    c["MIf"] = (p >= k).astype(np.float32)
    c["MIb"] = (p <= k).astype(np.float32)
    src = np.concatenate([np.arange(32, 64), np.arange(0, 32), np.arange(96, 128), np.arange(64, 96)])
    P = np.zeros((128, 128), np.float32)
    P[src, np.arange(128)] = 1.0
    c["permT"] = P
    return c


def _rope_tables(pos):
    half = 64
    inv = (1.0 / (10000.0 ** (np.arange(0, half, 2, dtype=np.float32) / half))).astype(np.float32)
    r = (pos // 64).astype(np.float32)[:, None]
    col = (pos % 64).astype(np.float32)[:, None]
    ang = np.concatenate([r * inv, r * inv, col * inv, col * inv], axis=-1).astype(np.float32)
    return np.cos(ang).T.astype(np.float32), np.sin(ang).T.astype(np.float32)


def _pj(v, nj):
    return np.ascontiguousarray(np.asarray(v, np.float32).reshape(nj, 128).T)


def kernel(x, c, ctx, c_ctx, w_ada, b_ada, g_mix_pre, g_mix_post, g_ffn_pre, g_ffn_post,
           w_ff_gate, w_ff_up, w_ff_down, w_in_even, w_out_even, gdn_conv, gdn_a_log,
           gdn_dt_bias, gdn_norm, diff_lambda, diff_norm, w_in_odd, w_out_odd, conf_dw,
           conf_dw_b, conf_ln_g, conf_ln_b, sc_conv):
    in_maps, frames = prepare(**{k: v for k, v in locals().items()})
    nc = build_program()
    res = run_bass_kernel_spmd(nc, in_maps, core_ids=list(range(8)))
    out = np.zeros((4, 4096, 4096), np.float32)
    for core in range(8):
        b = core // 2
        oT = np.asarray(res.results[core]["outT"], np.float32)
        out[b, frames[core][:2048], :] = oT[:, :2048].T
    return out


def prepare(x, c, ctx, c_ctx, w_ada, b_ada, g_mix_pre, g_mix_post, g_ffn_pre, g_ffn_post,
            w_ff_gate, w_ff_up, w_ff_down, w_in_even, w_out_even, gdn_conv, gdn_a_log,
            gdn_dt_bias, gdn_norm, diff_lambda, diff_norm, w_in_odd, w_out_odd, conf_dw,
            conf_dw_b, conf_ln_g, conf_ln_b, sc_conv, cores=range(8)):
    f32 = lambda a: np.ascontiguousarray(np.asarray(a, np.float32))
    x = f32(x); ctx = f32(ctx)
    cst = _consts()
    sign = np.ones(128, np.float32); sign[0:32] = -1; sign[64:96] = -1
    shared = dict(w_ada=f32(w_ada), w_ff_gate=f32(w_ff_gate), w_ff_up=f32(w_ff_up), w_ff_down=f32(w_ff_down),
                  w_in_even=f32(w_in_even[0]), w_out_even=f32(w_out_even[0]), w_in_odd=f32(w_in_odd[0]), w_out_odd=f32(w_out_odd[0]))
    shared["bada"] = np.stack([_pj(b_ada[l], 192) for l in range(2)])
    shared["gains"] = np.stack([np.stack([_pj(g[l], 32) for g in (g_mix_pre, g_mix_post, g_ffn_pre, g_ffn_post)]) for l in range(2)])
    shared["gnorm"] = f32(gdn_norm[0]).reshape(128, 1)
    shared["lamv"] = np.ascontiguousarray(f32(diff_lambda[0]).T)
    shared["dnorm"] = _pj(diff_norm[0], 2)
    shared["conf_vec"] = np.ascontiguousarray(np.stack([_pj(conf_dw_b[0], 16), _pj(conf_ln_g[0], 16), _pj(conf_ln_b[0], 16)], axis=-1))
    for k_ in ("ident", "Uf", "Ub", "MSf", "MSb", "MIf", "MIb", "permT"):
        shared[k_] = cst[k_]
    in_maps = []
    frames = []
    gconv = f32(gdn_conv[0])
    cdw = f32(conf_dw[0])
    scw = f32(sc_conv[0])
    wab = f32(w_in_even[0][:, 8192:8256])
    for core in cores:
        b, s = core // 2, core % 2
        idx = np.arange(4096) if s == 0 else np.arange(4095, -1, -1)
        cidx = np.arange(256) if s == 0 else np.arange(255, -1, -1)
        own, oth = idx[:TO], idx[TO:]
        frames.append(own)
        m = dict(shared)
        m["xT_own"] = np.ascontiguousarray(x[b, own, :].T)
        m["xT_oth"] = np.ascontiguousarray(x[b, oth, :].T)
        m["ctxT"] = np.ascontiguousarray(ctx[b, cidx, :].T)
        cv = np.stack([f32(c[b]), f32(c_ctx)], axis=-1)
        m["cvec"] = np.ascontiguousarray(cv.reshape(32, 128, 2).transpose(1, 0, 2))
        taps = gconv if s == 0 else gconv[::-1]
        m["convw"] = np.ascontiguousarray(taps.reshape(3, 3, 16, 128).transpose(3, 1, 2, 0).reshape(128, 144))
        al = f32(gdn_a_log[0]); db = f32(gdn_dt_bias[0])
        w_ab = wab
        if s == 1:
            al = al[::-1]; db = db[::-1]
            w_ab = np.concatenate([wab[:, 16:32], wab[:, 0:16], wab[:, 48:64], wab[:, 32:48]], axis=1)
        m["alog"] = np.ascontiguousarray(np.broadcast_to(al.reshape(1, 32), (128, 32)))
        m["dtb"] = np.ascontiguousarray(np.broadcast_to(db.reshape(1, 32), (128, 32)))
        m["w_ab"] = np.ascontiguousarray(w_ab)
        cd = cdw if s == 0 else cdw[::-1]
        m["conf_dw"] = np.ascontiguousarray(cd.reshape(31, 16, 128).transpose(2, 1, 0))
        sw = scw if s == 0 else scw[::-1]
        m["sc_w"] = np.ascontiguousarray(sw.reshape(3, 16, 128).transpose(2, 1, 0))
        cosl, sinl = _rope_tables(idx)
        sc_ = np.float32(128 ** -0.5)
        m["cosq"] = np.ascontiguousarray(cosl[:, :TO] * sc_)
        m["sinq"] = np.ascontiguousarray(sinl[:, :TO] * sign[:, None] * sc_)
        m["cosk"] = np.ascontiguousarray(np.concatenate([cosl, np.ones((128, TC), np.float32)], axis=1))
        m["sink"] = np.ascontiguousarray(np.concatenate([sinl * sign[:, None], np.zeros((128, TC), np.float32)], axis=1))
        in_maps.append(m)
    return in_maps, frames
```
